# Optimizing a Trainium2 kernel written in Bass

```python
import math
import jax, jax.numpy as jnp
from jax import lax
import numpy as np

D_MODEL = 1024
BATCH = 4
SEQ = 8192
DEPTH = 2
DEC_BATCH = 32
DEC_SEQ = 16
PAST_LEN = 1024

CHUNK = 64
MIX_WIDTH = D_MODEL
HG_WIDTH = MIX_WIDTH // 2
HG_HEAD_DIM = 128
HG_HEADS = HG_WIDTH // HG_HEAD_DIM
ATT_WIDTH = MIX_WIDTH - HG_WIDTH
ATT_HEAD_DIM = 64
ATT_HEADS = ATT_WIDTH // ATT_HEAD_DIM
LEFT_CHUNKS = 8
ATT_REACH = LEFT_CHUNKS * CHUNK
BAND = ATT_REACH + CHUNK
REL_CLIP = 128
N_REL = 2 * REL_CLIP + 1
D_FF = 4 * D_MODEL
IN_WIDTH = 4 * HG_WIDTH + 3 * ATT_WIDTH
SPLITS = [HG_WIDTH, 2 * HG_WIDTH, 3 * HG_WIDTH, 4 * HG_WIDTH,
          4 * HG_WIDTH + ATT_WIDTH, 4 * HG_WIDTH + 2 * ATT_WIDTH]
EPS = 1e-6
NEG = -1e30

kernel_name = 'hymba_hgrn2_chunkband_stream_step'


def rmsnorm(x, g):
    xf = x.astype(jnp.float32)
    y = xf * lax.rsqrt(jnp.mean(xf * xf, axis=-1, keepdims=True) + EPS)
    return (y * g.astype(jnp.float32)).astype(x.dtype)


def hgrn2_scan(q, k, v, logf, s0):
    B, T, H, DK = q.shape
    C = min(CHUNK, T)
    N = T // C

    def to_chunks(a):
        return a.reshape(B, N, C, H, a.shape[-1]).transpose(1, 0, 3, 2, 4)

    qc, kc, vc, gc = to_chunks(q), to_chunks(k), to_chunks(v), to_chunks(logf)
    causal = jnp.tril(jnp.ones((C, C), dtype=bool))[:, :, None]

    def step(S, inp):
        qb, kb, vb, gb = inp
        b = jnp.cumsum(gb, axis=2)
        inter = jnp.einsum('bhtk,bhkv->bhtv', qb * jnp.exp(b), S)
        diff = b[:, :, :, None, :] - b[:, :, None, :, :]
        decay = jnp.where(causal, jnp.exp(jnp.minimum(diff, 0.0)), 0.0)
        scores = jnp.einsum('bhtsk,bhsk->bhts', qb[:, :, :, None, :] * decay, kb)
        intra = jnp.einsum('bhts,bhsv->bhtv', scores, vb)
        b_last = b[:, :, -1:, :]
        S_new = (jnp.exp(b_last[:, :, 0, :])[..., None] * S
                 + jnp.einsum('bhsk,bhsv->bhkv', kb * jnp.exp(b_last - b), vb))
        return S_new, inter + intra

    S_fin, o = lax.scan(step, s0, (qc, kc, vc, gc))
    o = o.transpose(1, 0, 3, 2, 4).reshape(B, T, H, v.shape[-1])
    return o, S_fin


def band_attention(q, k, v, qpos, kpos, kvalid, rel_bias):
    rel = jnp.clip(kpos[None, :] - qpos[:, None], -REL_CLIP, REL_CLIP) + REL_CLIP
    bias = rel_bias.astype(jnp.float32)[:, rel]
    s = jnp.einsum('bqhd,bkhd->bhqk', q, k, preferred_element_type=jnp.float32)
    s = s * (1.0 / math.sqrt(ATT_HEAD_DIM)) + bias[None]
    s = jnp.where(kvalid[None, None, None, :], s, NEG)
    p = jax.nn.softmax(s, axis=-1)
    return jnp.einsum('bhqk,bkhd->bqhd', p.astype(v.dtype), v)


def prompt_attention(q, k, v, rel_bias):
    B, T, H, Dh = q.shape
    N = T // CHUNK
    pad = ((0, 0), (ATT_REACH, 0), (0, 0), (0, 0))
    kp = jnp.pad(k, pad)
    vp = jnp.pad(v, pad)
    qc = q.reshape(B, N, CHUNK, H, Dh).transpose(1, 0, 2, 3, 4)
    key_off = jnp.arange(BAND) - ATT_REACH
    q_off = jnp.arange(CHUNK)

    def one_chunk(args):
        c, qb = args
        start = c * CHUNK
        kb = lax.dynamic_slice_in_dim(kp, start, BAND, axis=1)
        vb = lax.dynamic_slice_in_dim(vp, start, BAND, axis=1)
        kpos = start + key_off
        return band_attention(qb, kb, vb, start + q_off, kpos, kpos >= 0, rel_bias)

    o = lax.map(one_chunk, (jnp.arange(N), qc))
    return o.transpose(1, 0, 2, 3, 4).reshape(B, T, H, Dh)


def trunk_layer(x, s0, k_past, v_past, lb, norm1_g, w_in, hg_norm_g, rel_bias,
                att_norm_g, w_out, norm2_g, w_up, w_down):
    B, T, _ = x.shape
    f32 = jnp.float32
    h = rmsnorm(x, norm1_g)
    proj = h @ w_in
    hq, hf, hi, hg, aq, ak, av = jnp.split(proj, SPLITS, axis=-1)

    fgate = lb + (1.0 - lb) * jax.nn.sigmoid(hf.astype(f32))
    logf = jnp.log(fgate)
    kk = 1.0 - fgate
    hd = lambda a: a.reshape(B, T, HG_HEADS, HG_HEAD_DIM)
    if s0 is None:
        s0 = jnp.zeros((B, HG_HEADS, HG_HEAD_DIM, HG_HEAD_DIM), f32)
    o_h, s_new = hgrn2_scan(hd(hq.astype(f32)), hd(kk), hd(hi.astype(f32)), hd(logf), s0.astype(f32))
    o_h = rmsnorm(o_h, hg_norm_g.reshape(HG_HEADS, HG_HEAD_DIM)).reshape(B, T, HG_WIDTH)
    o_h = (o_h * jax.nn.silu(hg.astype(f32))).astype(x.dtype)

    ad = lambda a: a.reshape(B, T, ATT_HEADS, ATT_HEAD_DIM)
    q, k, v = ad(aq), ad(ak), ad(av)
    if k_past is None:
        o_a = prompt_attention(q, k, v, rel_bias)
        keep = min(ATT_REACH, T)
        k_rows, v_rows = k[:, T - keep:], v[:, T - keep:]
    else:
        W = k_past.shape[1]
        k_all = jnp.concatenate([k_past.astype(k.dtype), k], axis=1)
        v_all = jnp.concatenate([v_past.astype(v.dtype), v], axis=1)
        kpos = jnp.arange(W + T) - W
        o_a = band_attention(q, k_all, v_all, jnp.arange(T), kpos,
                             jnp.ones((W + T,), dtype=bool), rel_bias)
        k_rows, v_rows = k, v
    o_a = rmsnorm(o_a.reshape(B, T, ATT_WIDTH), att_norm_g)

    x = x + jnp.concatenate([o_h, o_a.astype(x.dtype)], axis=-1) @ w_out
    u = rmsnorm(x, norm2_g) @ w_up
    x = x + jnp.square(jax.nn.relu(u)) @ w_down
    return x, s_new, k_rows, v_rows


def setup_inputs(seed: int = 0) -> dict:
    key = jax.random.key(seed)
    ks = jax.random.split(key, 16)
    n = lambda k, shape, s: (jax.random.normal(k, shape, jnp.float32) * s)
    att_cache_len = min(ATT_REACH, PAST_LEN)
    return {
        'x_prompt': n(ks[0], (BATCH, SEQ, D_MODEL), 1.0),
        'x_sample': n(ks[1], (DEC_BATCH, DEC_SEQ, D_MODEL), 1.0),
        'state_hgrn': n(ks[2], (DEPTH, DEC_BATCH, HG_HEADS, HG_HEAD_DIM, HG_HEAD_DIM), 0.5),
        'cache_k': n(ks[3], (DEPTH, DEC_BATCH, att_cache_len, ATT_HEADS, ATT_HEAD_DIM), 1.0),
        'cache_v': n(ks[4], (DEPTH, DEC_BATCH, att_cache_len, ATT_HEADS, ATT_HEAD_DIM), 1.0),
        'lb_param': n(ks[5], (DEPTH, HG_WIDTH), 0.5),
        'norm1_g': 1.0 + n(ks[6], (DEPTH, D_MODEL), 0.1),
        'w_in': n(ks[7], (DEPTH, D_MODEL, IN_WIDTH), D_MODEL ** -0.5),
        'hg_norm_g': 1.0 + n(ks[8], (DEPTH, HG_WIDTH), 0.1),
        'rel_bias': n(ks[9], (DEPTH, ATT_HEADS, N_REL), 0.5),
        'att_norm_g': 1.0 + n(ks[10], (DEPTH, ATT_WIDTH), 0.1),
        'w_out': n(ks[11], (DEPTH, MIX_WIDTH, D_MODEL), MIX_WIDTH ** -0.5),
        'norm2_g': 1.0 + n(ks[12], (DEPTH, D_MODEL), 0.1),
        'w_up': n(ks[13], (DEPTH, D_MODEL, D_FF), D_MODEL ** -0.5),
        'w_down': n(ks[14], (DEPTH, D_FF, D_MODEL), 0.5 * D_FF ** -0.5),
        'final_norm_g': 1.0 + n(ks[15], (D_MODEL,), 0.1),
    }


def reference(x_prompt, x_sample, state_hgrn, cache_k, cache_v, lb_param, norm1_g, w_in,
              hg_norm_g, rel_bias, att_norm_g, w_out, norm2_g, w_up, w_down, final_norm_g):
    sm = jax.nn.softmax(lb_param.astype(jnp.float32), axis=0)
    lbs = jnp.cumsum(sm, axis=0) - sm[0:1]
    xp, xs = x_prompt, x_sample
    sp_l, kp_l, vp_l, ss_l, ks_l, vs_l = [], [], [], [], [], []
    for l in range(DEPTH):
        lw = (lbs[l], norm1_g[l], w_in[l], hg_norm_g[l], rel_bias[l], att_norm_g[l],
              w_out[l], norm2_g[l], w_up[l], w_down[l])
        xp, sp, kp, vp = trunk_layer(xp, None, None, None, *lw)
        xs, ss, ksm, vsm = trunk_layer(xs, state_hgrn[l], cache_k[l], cache_v[l], *lw)
        sp_l.append(sp); kp_l.append(kp); vp_l.append(vp)
        ss_l.append(ss); ks_l.append(ksm); vs_l.append(vsm)
    y_prompt = rmsnorm(xp, final_norm_g)
    y_sample = rmsnorm(xs, final_norm_g)
    new_state_hgrn_prompt = jnp.stack(sp_l)
    new_cache_k_prompt = jnp.stack(kp_l)
    new_cache_v_prompt = jnp.stack(vp_l)
    new_state_hgrn_sample = jnp.stack(ss_l)
    new_cache_k_sample = jnp.stack(ks_l)
    new_cache_v_sample = jnp.stack(vs_l)
    return (y_prompt, y_sample, new_state_hgrn_prompt, new_cache_k_prompt, new_cache_v_prompt,
            new_state_hgrn_sample, new_cache_k_sample, new_cache_v_sample)
```

```python
import numpy as np
import ml_dtypes
from contextlib import ExitStack
import concourse.bass as bass
import concourse.mybir as mybir
from concourse.bass_utils import run_bass_kernel_spmd

F32 = mybir.dt.float32
BF16 = mybir.dt.bfloat16
AF = mybir.ActivationFunctionType
ALU = mybir.AluOpType
AX = mybir.AxisListType

D = 1024
DFF = 4096
EPS = 1e-6
NCH = 25
NSLOT = 4


class _Op:
    __slots__ = ("eng", "fn", "deps", "sig", "ord", "waits", "key", "idx", "dma", "ka", "dbg")


class Prog:
    ENGS = ("pe", "act", "dve", "pool", "sp")

    def __init__(self):
        self.ops = []
        self.lastw = {}
        self.readers = {}
        self.children = {}

    def _register(self, key):
        if key in self.children:
            return
        self.children[key] = set()
        for n in range(1, len(key)):
            pre = key[:n]
            if pre not in self.children:
                self._register(pre)
            self.children[pre].add(key)

    def _related(self, key):
        self._register(key)
        yield key
        for n in range(1, len(key)):
            yield key[:n]
        for c in self.children[key]:
            yield c

    limit = None
    marks = None

    def mark(self, name):
        if self.marks is not None:
            self.marks.append((name, len(self.ops)))

    def op(self, eng, fn, reads=(), writes=(), dma=None):
        if self.limit is not None and len(self.ops) >= self.limit:
            return None
        o = _Op()
        o.eng = eng
        o.fn = fn
        o.dma = dma
        o.key = ("dma", dma) if dma is not None else eng
        o.sig = dma is not None
        o.idx = len(self.ops)
        o.dbg = (tuple(reads), tuple(writes))
        writes = list(writes) + [r for r in reads if r[0] == "ps" and r not in writes]
        deps = {}
        for r in reads:
            for k in self._related(r):
                w = self.lastw.get(k)
                if w is not None:
                    deps[w.idx] = w
        for w_ in writes:
            for k in self._related(w_):
                w = self.lastw.get(k)
                if w is not None:
                    deps[w.idx] = w
                for rd in self.readers.get(k, ()):
                    deps[rd.idx] = rd
        o.deps = list(deps.values())
        for r in reads:
            lst = self.readers.setdefault(r, [])
            lst[:] = [q for q in lst if q.key != o.key]
            lst.append(o)
        for w_ in writes:
            self.lastw[w_] = o
            self.readers[w_] = []
        self.ops.append(o)
        return o

    def finalize(self):
        for o in self.ops:
            for d in o.deps:
                d.sig = True
        cnt = {}
        for o in self.ops:
            if o.sig:
                cnt[o.key] = cnt.get(o.key, 0) + 1
                o.ord = cnt[o.key]
            else:
                o.ord = 0
        self.final_counts = cnt
        prev = {e: {} for e in self.ENGS}
        for o in self.ops:
            k = dict(prev[o.eng])
            waits = []
            for d in sorted(o.deps, key=lambda d: -d.idx):
                if d.key == "pe" and o.eng == "pe" and o.dma is None:
                    continue
                if k.get(d.key, 0) >= d.ord:
                    continue
                waits.append((d.key, d.ord))
                for kk, vv in d.ka.items():
                    if k.get(kk, 0) < vv:
                        k[kk] = vv
            o.waits = waits
            prev[o.eng] = k
            ka = dict(k)
            if o.sig:
                ka[o.key] = o.ord
            o.ka = ka

    def emit(self, nc, es):
        keys = []
        for o in self.ops:
            if o.sig and o.key not in keys:
                keys.append(o.key)
        sems = {}
        for i, k in enumerate(keys):
            sems[k] = es.enter_context(nc.semaphore("s%d" % i))
        by = {e: [o for o in self.ops if o.eng == e] for e in self.ENGS}
        block = es.enter_context(nc.Block())

        def body(ename):
            def run(e):
                for o in by[ename]:
                    for (k, v) in o.waits:
                        e.wait_ge(sems[k], v * 16 if isinstance(k, tuple) else v)
                    ins = o.fn(e)
                    if o.sig:
                        ins.then_inc(sems[o.key], 16 if o.dma is not None else 1)
                if ename == "sp":
                    for k, v in self.final_counts.items():
                        if isinstance(k, tuple):
                            e.wait_ge(sems[k], v * 16)
            return run

        block.tensor(body("pe"))
        block.scalar(body("act"))
        block.vector(body("dve"))
        block.gpsimd(body("pool"))
        block.sync(body("sp"))
        return len(keys)


def build(NPREV, NOWN, NS=4):
    assert NS == 4
    nc = bass.Bass("TRN2", target_bir_lowering=False)
    NT = NPREV + NOWN
    P = Prog()
    import os as _os
    if _os.environ.get("KLIMIT"):
        P.limit = int(_os.environ["KLIMIT"])
    if _os.environ.get("KMARKS"):
        P.marks = []
    es = ExitStack()

    def din(name, shape, dt=F32):
        return nc.dram_tensor(name, list(shape), dt, kind="ExternalInput").ap()

    def dout(name, shape, dt=F32):
        return nc.dram_tensor(name, list(shape), dt, kind="ExternalOutput").ap()

    xp = din("xp", [NT * 512, D])
    xs = din("xs", [4, 128, D])
    st0 = din("st0", [2, 4, 4, 128, 128])
    ck = din("ck", [2, 4, 512, 512])
    cv = din("cv", [2, 4, 512, 512])
    lbp = din("lbp", [1, 1024])
    fgain = din("fgain", [1, D])
    rb0d = din("rb0d", [1, 16])
    biasm = din("biasm", [128, 2 * 8 * 2 * 128])
    g1 = din("g1", [128, 16])
    g2 = din("g2", [128, 16])
    go = din("go", [128, 16])
    pflagd = din("pflag", [128, 8])
    svald = din("sval", [128, 8])
    identd = din("ident", [128, 128], BF16)
    m1td = din("m1t", [128, 128])
    indd = din("ind", [128, 8])
    caustd = din("caust", [128, 64], BF16)
    w_in = din("w_in", [2, D, 3584])
    w_out = din("w_out", [2, D, D])
    w_up = din("w_up", [2, D, DFF])
    w_down = din("w_down", [2, DFF, D])

    y_o = dout("y", [NOWN * 512, D])
    ys_o = dout("ysamp", [4, 128, D])
    sp_o = dout("state_p", [2, 4, 128, 128])
    kp_o = dout("kc_p", [2, 512, 512])
    vp_o = dout("vc_p", [2, 512, 512])
    ss_o = dout("state_s", [2, 4, 4, 128, 128])
    ks_o = dout("kc_s", [2, 4, 128, 512])
    vs_o = dout("vc_s", [2, 4, 128, 512])

    wq = nc.dram_tensor("wq", [2 * NCH, 128, 4096], BF16, kind="Internal").ap()

    def sb(name, shape, dt=F32):
        return es.enter_context(nc.sbuf_tensor(name, list(shape), dt))

    x = sb("x", [128, 4, D])
    ystg = sb("ystg", [128, 1, D])
    xn = sb("xn", [128, 2, D], BF16)
    aT = sb("aT", [128, 8, 512], BF16)
    hq = sb("hq", [128, 4, 512])
    sig = sb("sig", [128, 4, 512])
    vh = sb("vh", [128, 4, 512], BF16)
    sg = sb("sg", [128, 4, 512], BF16)
    QT = sb("QT", [128, 4, 512], BF16)
    KR = [sb("KR%d" % l, [128, 4, 1024], BF16) for l in range(2)]
    VR = [sb("VR%d" % l, [128, 8, 520], BF16) for l in range(2)]
    lf = sb("lf", [128, 512])
    e1 = sb("e1", [128, 512])
    e2 = sb("e2", [128, 512])
    qtl = sb("qtl", [128, 512], BF16)
    ktl = sb("ktl", [128, 512], BF16)
    qkT = sb("qkT", [128, 1024], BF16)
    pt = sb("pt", [128, 4, 64], BF16)
    Sst = [sb("S%d" % l, [128, 512]) for l in range(2)]
    smid = sb("smid", [128, 512], BF16)
    stmp = sb("stmp", [128, 512])
    cE = sb("cE", [128, 32])
    PT = sb("PT", [128, 2, 5, 128], BF16)
    oa = sb("oa", [128, 8, 64])
    rden = sb("rden", [128, 8])
    oh = sb("oh", [128, 512])
    mb = sb("mb", [128, 2, D], BF16)
    ko, vo = stmp, oh
    small = sb("small", [128, 64])
    arena = sb("arena", [128, 16384], BF16)
    rl = sb("rl", [128, 1, 512])
    wr = sb("wr", [128, NSLOT, 4096], BF16)
    ident = sb("identb", [128, 128], BF16)
    m1t = sb("m1tb", [128, 128])
    ind = sb("indb", [128, 8])
    caust = sb("caustb", [128, 64], BF16)
    lbt = sb("lbt", [128, 512])
    oml = sb("oml", [128, 2, 512])
    fgn = sb("fgn", [128, D])
    rb0 = sb("rb0", [128, 16])
    enrb = sb("enrb", [128, 16])
    ebt = sb("ebt", [128, 2 * 8 * 2 * 128], BF16)
    g1s = sb("g1s", [128, 16])
    g2s = sb("g2s", [128, 16])
    gos = sb("gos", [128, 16])
    pflag = sb("pflags", [128, 8])
    sval = sb("svals", [128, 8])
    ones8 = sb("ones8", [128, 8])
    epsc = sb("epsc", [128, 8])

    hT = arena[:, :].rearrange("p (a b) -> p a b", a=32)
    arena_f = arena[:, :].bitcast(F32)
    bT = aT
    Ssm = arena_f[:, 2048:4096].rearrange("p (s c) -> p s c", s=4)
    KN = arena[:, 10240:12288].rearrange("p (a b) -> p a b", a=4)
    VN = arena[:, 12288:14368].rearrange("p (a b) -> p a b", a=4)

    def hkey(ft):
        return ("ar", ft // 8, ft % 8)

    ps = [es.enter_context(nc.psum_tensor("ps%d" % b, [128, 512], F32)) for b in (0, 1)]
    pT = es.enter_context(nc.psum_tensor("pT", [128, 1024], BF16))
    ps += [None]
    ps += [es.enter_context(nc.psum_tensor("ps%d" % b, [128, 512], F32)) for b in (3, 4, 5, 6, 7)]
    PSK = lambda b, *r: ("ps", b)

    rot = {"n": 0}

    def nextbank():
        rot["n"] ^= 1
        return rot["n"]

    tog = {"n": 0}

    def evac_eng():
        tog["n"] ^= 1
        return "act" if tog["n"] else "dve"

    def copy_op(eng, out, in_, reads, writes, scale=None):
        if eng == "act":
            if scale is None:
                P.op("act", lambda e: e.activation(out=out, in_=in_, func=AF.Copy), reads, writes)
            else:
                P.op("act", lambda e: e.activation(out=out, in_=in_, func=AF.Copy, scale=scale), reads, writes)
        else:
            if scale is None:
                P.op("dve", lambda e: e.tensor_copy(out=out, in_=in_), reads, writes)
            else:
                P.op("dve", lambda e: e.tensor_scalar(out=out, in0=in_, scalar1=scale, scalar2=None,
                                                      op0=ALU.mult), reads, writes)

    def cload(dst, src, key, q="sp"):
        P.op(q, lambda e: e.dma_start(out=dst, in_=src), [], [key], dma=("c",) + key)

    cload(ident[:, :], identd[:, :], ("ident",))
    cload(m1t[:, :], m1td[:, :], ("m1t",))
    cload(ind[:, :], indd[:, :], ("ind",))
    cload(caust[:, :], caustd[:, :], ("caust",))
    cload(g1s[:, :], g1[:, :], ("g1s",))
    cload(g2s[:, :], g2[:, :], ("g2s",))
    cload(gos[:, :], go[:, :], ("gos",))
    cload(pflag[:, :], pflagd[:, :], ("pflag",))
    cload(sval[:, :], svald[:, :], ("sval",))
    cload(fgn[:, :], bass.AP(fgain.tensor, 0, [[0, 128], [1, D]]), ("fgn",))
    cload(rb0[:, :], bass.AP(rb0d.tensor, 0, [[0, 128], [1, 16]]), ("rb0",))
    cload(oml[:, :, :].rearrange("p a b -> p (a b)"), bass.AP(lbp.tensor, 0, [[0, 128], [1, 1024]]), ("oml",))
    P.op("dve", lambda e: e.tensor_tensor(out=lbt[:, :], in0=oml[:, 1, :], in1=oml[:, 0, :], op=ALU.subtract),
         [("oml",)], [("lbt",)])
    P.op("act", lambda e: e.activation(out=lbt[:, :], in_=lbt[:, :], func=AF.Sigmoid), [("lbt",)], [("lbt",)])
    P.op("dve", lambda e: e.tensor_scalar(out=oml[:, 1, :], in0=lbt[:, :], scalar1=-1.0, scalar2=1.0,
                                          op0=ALU.mult, op1=ALU.add), [("lbt",)], [("oml",)])
    P.op("dve", lambda e: e.memset(ones8[:, :], 1.0), [], [("ones8",)])
    P.op("dve", lambda e: e.memset(epsc[:, :], EPS), [], [("epsc",)])
    P.op("act", lambda e: e.activation(out=enrb[:, :], in_=rb0[:, :], func=AF.Exp, scale=-1.0), [("rb0",)], [("enrb",)])
    for hlf in range(2):
        c0 = hlf * 2048
        P.op("sp", lambda e, c0=c0: e.dma_start(out=arena_f[:, 0:2048], in_=biasm[:, c0:c0 + 2048]),
             [], [("ar", 0)], dma=("bstg",))
        P.op("act", lambda e, c0=c0: e.activation(out=ebt[:, c0:c0 + 2048], in_=arena_f[:, 0:2048], func=AF.Exp),
             [("ar", 0)], [("ebt",)])
        for gi in range(8):
            col = hlf * 8 + gi
            o3 = c0 + gi * 256
            P.op("dve", lambda e, o3=o3, col=col: e.tensor_scalar(
                out=ebt[:, o3:o3 + 128], in0=ebt[:, o3:o3 + 128], scalar1=enrb[:, col:col + 1], scalar2=None, op0=ALU.mult),
                [("ebt",), ("enrb",)], [("ebt",)])
    for l in range(2):
        P.op("dve", lambda e, l=l: e.memset(KR[l][:, :, :].rearrange("p a b -> p (a b)"), 0.0), [], [("KR", l)])
        P.op("dve", lambda e, l=l: e.memset(VR[l][:, :, :].rearrange("p a b -> p (a b)"), 0.0), [], [("VR", l)])
        P.op("dve", lambda e, l=l: e.memset(Sst[l][:, :], 0.0), [], [("S", l)])
    P.op("dve", lambda e: e.memset(PT[:, :, :, :].rearrange("p a b c -> p (a b c)"), 0.0), [], [("PT",)])

    P.mark("prologue")
    pcount = {"n": 0}

    pieces = []

    def conv_piece(src_ap, ncols, gain_ap, dst_ap, dkey):
        pieces.append((src_ap, ncols, gain_ap, dst_ap, dkey))

    def emit_pieces():
        n = len(pieces)
        for t in range(n + 3):
            if t < n:
                src_ap, ncols, gain_ap, dst_ap, dkey = pieces[t]
                i = t % 4
                stg = arena_f[:, i * 2048: i * 2048 + ncols]
                P.op("sp", lambda e, stg=stg, src_ap=src_ap: e.dma_start(out=stg, in_=src_ap), [], [("ar", i)], dma=("pl", i))
            u = t - 3
            if u >= 0:
                src_ap, ncols, gain_ap, dst_ap, dkey = pieces[u]
                i = u % 4
                stg = arena_f[:, i * 2048: i * 2048 + ncols]
                ob = wr[:, i, 0:ncols]
                eng = "act" if i % 2 == 0 else "dve"
                if gain_ap is None:
                    copy_op(eng, ob, stg, [("ar", i)], [("wr", i)])
                else:
                    copy_op(eng, ob, stg, [("ar", i), ("gains",)], [("wr", i)], scale=gain_ap)
                P.op("sp", lambda e, ob=ob, dst_ap=dst_ap: e.dma_start(out=dst_ap, in_=ob), [("wr", i)], [dkey], dma=("pst", i))

    P.op("dve", lambda e: e.tensor_copy(out=small[:, 60:61], in_=g1s[:, 0:1]), [("g1s",), ("g2s",), ("gos",)], [("gains",)])
    for l in range(2):
        for dc in range(8):
            rows = w_in[l, dc * 128:(dc + 1) * 128, :]
            for (c0, ncg) in ((0, 4), (4, 3)):
                dst = wq[l * NCH + c0: l * NCH + c0 + ncg, :, dc * 512:(dc + 1) * 512].rearrange("c p k -> p c k")
                conv_piece(rows[:, c0 * 512:(c0 + ncg) * 512], ncg * 512, g1s[:, l * 8 + dc: l * 8 + dc + 1],
                           dst, ("wq", l, "in", dc, c0))
        for cc in range(8):
            rows = w_out[l, cc * 128:(cc + 1) * 128, :]
            dst = wq[l * NCH + 7: l * NCH + 9, :, cc * 512:(cc + 1) * 512].rearrange("c p k -> p c k")
            conv_piece(rows, 1024, gos[:, l * 8 + cc: l * 8 + cc + 1], dst, ("wq", l, "out", cc))
        for dc in range(8):
            rows = w_up[l, dc * 128:(dc + 1) * 128, :]
            for hf in range(2):
                dst = wq[l * NCH + 9 + hf * 4: l * NCH + 13 + hf * 4, :, dc * 512:(dc + 1) * 512].rearrange("c p k -> p c k")
                conv_piece(rows[:, hf * 2048:(hf + 1) * 2048], 2048, g2s[:, l * 8 + dc: l * 8 + dc + 1],
                           dst, ("wq", l, "up", dc, hf))
        for ffc in range(32):
            rows = w_down[l, ffc * 128:(ffc + 1) * 128, :]
            fg, fc = ffc // 8, ffc % 8
            dst = wq[l * NCH + 17 + fg: l * NCH + 17 + fg + 5: 4, :, fc * 512:(fc + 1) * 512].rearrange("c p k -> p c k")
            conv_piece(rows, 1024, None, dst, ("wq", l, "down", ffc))

    emit_pieces()

    wst = {"n": 0}

    def wchunk(l, ci):
        slot = wst["n"] % NSLOT
        wst["n"] += 1
        P.op("sp", lambda e: e.dma_start(out=wr[:, slot, :], in_=wq[l * NCH + ci, :, :]),
             [("wq", l)], [("wr", slot)], dma=("w", slot))
        return wr[:, slot, :].rearrange("p (a b) -> p a b", a=8), ("wr", slot)

    scol = {"n": 0}

    def newcol(n=1):
        c = scol["n"]
        scol["n"] = (scol["n"] + n) % 48
        if c + n > 48:
            c = 0
            scol["n"] = n
        return c

    def rstd_chain(ss_ap, key, n, inv_n):
        P.op("act", lambda e: e.activation(out=ss_ap, in_=ss_ap, func=AF.Ln, scale=inv_n, bias=epsc[:, 0:1]),
             [key, ("epsc",)], [key])
        P.op("act", lambda e: e.activation(out=ss_ap, in_=ss_ap, func=AF.Exp, scale=-0.5), [key], [key])

    def norm_T(s, dstT, dkey):
        i = s % 2
        c = newcol()
        sk = ("small", c)
        ssc = small[:, c:c + 1]
        P.op("act", lambda e: e.activation(out=xn[:, i, :], in_=x[:, s, :], func=AF.Square, accum_out=ssc),
             [("x", s)], [("xn", i), sk])
        rstd_chain(ssc, sk, 1, 1.0 / D)
        P.op("dve", lambda e: e.tensor_scalar(out=xn[:, i, :], in0=x[:, s, :], scalar1=ssc, scalar2=None,
                                              op0=ALU.mult), [("x", s), sk], [("xn", i)])

        def tr(e):
            for dc in range(8):
                ins = e.transpose(pT[:, dc * 128:(dc + 1) * 128], xn[:, i, dc * 128:(dc + 1) * 128], ident[:, :])
            return ins
        P.op("pe", tr, [("xn", i), ("ident",)], [("pT",)])
        dst = dstT[:, :, s * 128:(s + 1) * 128]
        src = pT[:, :].rearrange("p (a b) -> p a b", a=8)
        copy_op(evac_eng(), dst, src, [("pT",)], [dkey + (s,)])

    def tok_major(w, wkey, s):
        b = nextbank()

        def mm(e):
            for dc in range(8):
                ins = e.matmul(ps[b][:, :], lhsT=aT[:, dc, s * 128:(s + 1) * 128], rhs=w[:, dc, :],
                               start=(dc == 0), stop=(dc == 7))
            return ins
        P.op("pe", mm, [("aT", s), wkey], [PSK(b)])
        return b

    def feat_major(w, wkey, p):
        b = nextbank()

        def mm(e):
            for dc in range(8):
                ins = e.matmul(ps[b][:, :], lhsT=w[:, dc, p * 128:(p + 1) * 128], rhs=aT[:, dc, :],
                               start=(dc == 0), stop=(dc == 7))
            return ins
        P.op("pe", mm, [("aT",), wkey], [PSK(b)])
        return b

    def layer(l, ctx):
        light = ctx["light"]
        sample = ctx["sample"]
        g0 = ctx["g0"]
        P.mark("layer%d g0=%d sample=%s light=%s" % (l, g0, sample, light))

        if sample:
            for s in range(4):
                P.op("dve", lambda e, s=s: e.tensor_scalar(out=x[:, s, :], in0=x[:, s, :], scalar1=sval[:, 0:1], scalar2=None,
                                                           op0=ALU.mult), [("x", s), ("sval",)], [("x", s)])
        for s in range(4):
            norm_T(s, aT, ("aT",))

        if sample:
            def kdst(p):
                return KN[:, p, :], ("ar", 2)

            def vdst(s):
                return VN[:, s, :], ("ar", 3)
        else:
            slot0 = g0 % 8

            def kdst(p):
                return (KR[l][:, p, slot0 * 128:(slot0 + 4) * 128], ("KR", l, slot0 // 4, p))

            def vdst(s):
                return VR[l][:, slot0 + s, :], ("VR", l, slot0 + s)

        for cg in range(7):
            if light and cg in (0, 3, 4):
                continue
            if light and not ctx.get("need_kv", True) and cg in (5, 6):
                continue
            w, wkey = wchunk(l, cg)
            if cg in (4, 5):
                for p in range(4):
                    b = feat_major(w, wkey, p)
                    if cg == 4:
                        copy_op("act", QT[:, p, :], ps[b][:, :], [PSK(b)], [("QT", p)], scale=0.125)
                    else:
                        d, dk = kdst(p)
                        copy_op("dve", d, ps[b][:, :], [PSK(b)], [dk])
                if cg == 5 and ctx["kvout"] is not None:
                    for s in range(4):
                        dst = ctx["kvout"]["k"](l, s)
                        if dst is None:
                            continue
                        b = tok_major(w, wkey, s)
                        copy_op("act", ko[:, :], ps[b][:, :], [PSK(b)], [("stmp",)])
                        P.op("act", lambda e, dst=dst: e.dma_start(out=dst, in_=ko[:, :]), [("stmp",)], [], dma=("ko",))
            else:
                for s in range(4):
                    b = tok_major(w, wkey, s)
                    if cg == 0:
                        copy_op("act", hq[:, s, :], ps[b][:, :], [PSK(b)], [("hq", s)])
                    elif cg == 1:
                        P.op("act", lambda e, b=b, s=s: e.activation(out=sig[:, s, :], in_=ps[b][:, :], func=AF.Sigmoid),
                             [PSK(b)], [("sig", s)])
                    elif cg == 2:
                        copy_op("dve", vh[:, s, :], ps[b][:, :], [PSK(b)], [("vh", s)])
                    elif cg == 3:
                        P.op("act", lambda e, b=b, s=s: e.activation(out=sg[:, s, :], in_=ps[b][:, :], func=AF.Silu),
                             [PSK(b)], [("sg", s)])
                    elif cg == 6:
                        d, dk = vdst(s)
                        dv = d.rearrange("p (h c) -> p h c", h=8)
                        copy_op("dve", dv[:, :, 0:64], ps[b][:, :].rearrange("p (h c) -> p h c", h=8), [PSK(b)], [dk])
                        vf, vfk = ctx["vflag"]
                        P.op("dve", lambda e, dv=dv, vf=vf: e.tensor_copy(out=dv[:, :, 64], in_=vf), [vfk], [dk])
                        if ctx["kvout"] is not None:
                            dst = ctx["kvout"]["v"](l, s)
                            if dst is not None:
                                copy_op("act", vo[:, :], ps[b][:, :], [PSK(b)], [("oh",)])
                                P.op("act", lambda e, dst=dst: e.dma_start(out=dst, in_=vo[:, :]), [("oh",)], [], dma=("vo",))

        P.mark("mixers")
        def hgrn_part(s):
            if ctx.get("pre_mix") is not None:
                ctx["pre_mix"](l, s)
            S_ap, S_key = ctx["S"](l, s)
            par = s % 2
            if l == 1:
                P.op("dve", lambda e, s=s: e.tensor_tensor(out=sig[:, s, :], in0=sig[:, s, :], in1=oml[:, 1, :], op=ALU.mult),
                     [("sig", s), ("oml",)], [("sig", s)])
                P.op("dve", lambda e, s=s: e.tensor_tensor(out=sig[:, s, :], in0=sig[:, s, :], in1=lbt[:, :], op=ALU.add),
                     [("sig", s), ("lbt",)], [("sig", s)])
            P.op("act", lambda e, s=s: e.activation(out=lf[:, :], in_=sig[:, s, :], func=AF.Ln), [("sig", s)], [("lf",)])
            if sample:
                P.op("dve", lambda e: e.tensor_scalar(out=lf[:, :], in0=lf[:, :], scalar1=sval[:, 0:1], scalar2=None,
                                                      op0=ALU.mult), [("lf",), ("sval",)], [("lf",)])
            P.op("dve", lambda e, s=s: e.tensor_scalar(out=sig[:, s, :], in0=sig[:, s, :], scalar1=-1.0, scalar2=1.0,
                                                       op0=ALU.mult, op1=ALU.add), [("sig", s)], [("sig", s)])
            P.op("pe", lambda e: e.matmul(ps[3][:, :], lhsT=m1t[:, :], rhs=lf[:, :], start=True, stop=True),
                 [("m1t",), ("lf",)], [PSK(3)])

            def colmm(e):
                for h in range(4):
                    ins = e.matmul(ps[4][:, 256 + h * 8: 256 + h * 8 + 8], lhsT=lf[:, h * 128:(h + 1) * 128],
                                   rhs=ind[:, 0:8], start=True, stop=True)
                return ins
            P.op("pe", colmm, [("lf",), ("ind",)], [PSK(4, "cols")])
            P.op("act", lambda e: e.activation(out=cE[:, :], in_=ps[4][:, 256:288], func=AF.Exp), [PSK(4, "cols")], [("cE",)])
            P.op("act", lambda e: e.activation(out=e2[:, :], in_=ps[3][:, :], func=AF.Exp, scale=-1.0), [PSK(3)], [("e2",)])
            P.op("dve", lambda e, s=s: e.tensor_tensor(out=ktl[:, :], in0=sig[:, s, :], in1=e2[:, :], op=ALU.mult),
                 [("sig", s), ("e2",)], [("ktl",)])
            if not light:
                P.op("act", lambda e: e.activation(out=e1[:, :], in_=ps[3][:, :], func=AF.Exp), [PSK(3)], [("e1",)])
                P.op("dve", lambda e, s=s: e.tensor_tensor(out=qtl[:, :], in0=hq[:, s, :], in1=e1[:, :], op=ALU.mult),
                     [("hq", s), ("e1",)], [("qtl",)])

                def trq(e):
                    for h in range(4):
                        e.transpose(pT[:, h * 128:(h + 1) * 128], qtl[:, h * 128:(h + 1) * 128], ident[:, :])
                    for h in range(4):
                        ins = e.transpose(pT[:, 512 + h * 128: 512 + (h + 1) * 128], ktl[:, h * 128:(h + 1) * 128], ident[:, :])
                    return ins
                P.op("pe", trq, [("qtl",), ("ktl",), ("ident",)], [("pT",)])
                copy_op("act", qkT[:, :], pT[:, :], [("pT",)], [("qkT",)])

                def scm(e):
                    for h in range(4):
                        for c in range(2):
                            ins = e.matmul(ps[4][c * 64:(c + 1) * 64, h * 64:(h + 1) * 64],
                                           lhsT=qkT[:, 512 + h * 128 + c * 64: 512 + h * 128 + (c + 1) * 64],
                                           rhs=qkT[:, h * 128 + c * 64: h * 128 + (c + 1) * 64],
                                           start=True, stop=True, tile_position=(0, c * 64))
                    return ins
                P.op("pe", scm, [("qkT",)], [PSK(4, "sc")])
                P.op("dve", lambda e: e.tensor_tensor(
                    out=pt[:, :, :], in0=ps[4][:, 0:256].rearrange("p (h t) -> p h t", h=4),
                    in1=caust[:, :].unsqueeze(1).to_broadcast([128, 4, 64]), op=ALU.mult),
                    [PSK(4, "sc"), ("caust",)], [("pt",)])
            cE3 = cE[:, :].rearrange("p (h k) -> p h k", h=4)
            for c in range(2):
                if not light:
                    P.op("dve", lambda e, c=c, S_ap=S_ap: e.tensor_tensor(
                        out=smid[:, :].rearrange("p (h v) -> p h v", h=4), in0=S_ap.rearrange("p (h v) -> p h v", h=4),
                        in1=cE[:, :].rearrange("p (h k) -> p h k", h=4)[:, :, c * 3].unsqueeze(2).to_broadcast([128, 4, 128]),
                        op=ALU.mult), [S_key, ("cE",)], [("smid",)])

                    def omm(e, c=c, s=s):
                        for h in range(4):
                            e.matmul(ps[5][c * 64:(c + 1) * 64, h * 128:(h + 1) * 128],
                                     lhsT=qkT[:, h * 128 + c * 64: h * 128 + (c + 1) * 64],
                                     rhs=smid[:, h * 128:(h + 1) * 128], start=True, stop=False,
                                     tile_position=(0, c * 64))
                            ins = e.matmul(ps[5][c * 64:(c + 1) * 64, h * 128:(h + 1) * 128],
                                           lhsT=pt[c * 64:(c + 1) * 64, h, :],
                                           rhs=vh[c * 64:(c + 1) * 64, s, h * 128:(h + 1) * 128], start=False, stop=True,
                                           tile_position=(c * 64, c * 64))
                        return ins
                    P.op("pe", omm, [("qkT",), ("smid",), ("pt",), ("vh", s)], [PSK(5, c)])

                def sum_(e, c=c, s=s):
                    for h in range(4):
                        ins = e.matmul(ps[6][:, h * 128:(h + 1) * 128], lhsT=ktl[c * 64:(c + 1) * 64, h * 128:(h + 1) * 128],
                                       rhs=vh[c * 64:(c + 1) * 64, s, h * 128:(h + 1) * 128], start=True, stop=True)
                    return ins
                P.op("pe", sum_, [("ktl",), ("vh", s)], [PSK(6)])
                S3 = S_ap.rearrange("p (h v) -> p h v", h=4)
                P.op("dve", lambda e, c=c, S3=S3: e.tensor_tensor(
                    out=stmp[:, :].rearrange("p (h v) -> p h v", h=4), in0=S3,
                    in1=cE3[:, :, c * 3 + 2].unsqueeze(2).to_broadcast([128, 4, 128]), op=ALU.mult),
                    [S_key, ("cE",)], [("stmp",)])
                P.op("dve", lambda e, c=c, S3=S3: e.tensor_tensor(
                    out=S3, in0=ps[6][:, :].rearrange("p (h v) -> p h v", h=4),
                    in1=cE3[:, :, c * 3 + 1].unsqueeze(2).to_broadcast([128, 4, 128]), op=ALU.mult),
                    [PSK(6), ("cE",)], [S_key])
                P.op("dve", lambda e, S_ap=S_ap: e.tensor_tensor(out=S_ap, in0=S_ap, in1=stmp[:, :], op=ALU.add),
                     [S_key, ("stmp",)], [S_key])
            if light:
                return
            c4 = newcol(4)
            k4 = ("small", c4)
            ss4 = small[:, c4:c4 + 4]

            def sq4(e, c4=c4):
                for h in range(4):
                    ins = e.activation(out=oh[:, h * 128:(h + 1) * 128], in_=ps[5][:, h * 128:(h + 1) * 128],
                                       func=AF.Square, accum_out=small[:, c4 + h: c4 + h + 1])
                return ins
            P.op("act", sq4, [PSK(5)], [("oh",), k4])
            rstd_chain(ss4, k4, 4, 1.0 / 128)
            P.op("dve", lambda e, ss4=ss4: e.tensor_tensor(
                out=oh[:, :].rearrange("p (h v) -> p h v", h=4), in0=ps[5][:, :].rearrange("p (h v) -> p h v", h=4),
                in1=ss4.unsqueeze(2).to_broadcast([128, 4, 128]), op=ALU.mult), [PSK(5), k4], [("oh",)])
            P.op("dve", lambda e, s=s, par=par: e.tensor_tensor(out=mb[:, par, 0:512], in0=oh[:, :], in1=sg[:, s, :], op=ALU.mult),
                 [("oh",), ("sg", s)], [("mb", par, 0)])

        def attn_part(s):
            par = s % 2
            P.mark("attn s=%d" % s)
            g = g0 + s
            if sample:
                def ktile(p, j, s=s):
                    if j < 4:
                        sl = (s % 2) * 4 + j
                        return KR[l][:, p, sl * 128:(sl + 1) * 128], ("KR", l, sl // 4)
                    return KN[:, p, s * 128:(s + 1) * 128], ("ar", 2)

                def vtile(j, s=s):
                    if j < 4:
                        sl = (s % 2) * 4 + j
                        return VR[l][:, sl, :], ("VR", l, sl)
                    return VN[:, s, :], ("ar", 3)
            else:
                def ktile(p, j, g=g):
                    sl = (g - 4 + j) % 8
                    return KR[l][:, p, sl * 128:(sl + 1) * 128], ("KR", l, sl // 4)

                def vtile(j, g=g):
                    sl = (g - 4 + j) % 8
                    return VR[l][:, sl, :], ("VR", l, sl)
            SB = ((7, 1), (7, 1))

            def emit_scores(h, s=s):
                p, eo = h // 2, (h % 2) * 64
                ba, bt = SB[h % 2]
                kts = [ktile(p, j) for j in range(5)]

                def scmm(e, kts=kts, p=p, eo=eo, s=s, ba=ba, bt=bt):
                    for j in range(4):
                        e.matmul(ps[ba][:, j * 128:(j + 1) * 128], lhsT=kts[j][0][eo:eo + 64, :],
                                 rhs=QT[eo:eo + 64, p, s * 128:(s + 1) * 128], start=True, stop=True)
                    return e.matmul(ps[bt][:, 0:128], lhsT=kts[4][0][eo:eo + 64, :],
                                    rhs=QT[eo:eo + 64, p, s * 128:(s + 1) * 128], start=True, stop=True)
                P.op("pe", scmm, [k_[1] for k_ in kts] + [("QT", p)], [PSK(ba), PSK(bt)])

            emit_scores(0)
            for hg in range(2):
                for hh in range(4):
                    h = hg * 4 + hh
                    pp = h % 2
                    ba, bt = SB[h % 2]
                    bcol = l * 8 + h

                    def exps(e, pp=pp, bcol=bcol, ba=ba, bt=bt):
                        e.activation(out=PT[:, pp, 0:4, :], in_=ps[ba][:, 0:512].rearrange("p (a b) -> p a b", a=4),
                                     func=AF.Exp, bias=rb0[:, bcol:bcol + 1])
                        return e.activation(out=PT[:, pp, 4, :], in_=ps[bt][:, 0:128], func=AF.Exp)
                    P.op("act", exps, [PSK(ba), PSK(bt), ("rb0",)], [("PT", pp)])
                    P.op("dve", lambda e, pp=pp: e.memset(PT[0:64, pp, 0, 64:128], 0.0), [("PT", pp)], [("PT", pp)])
                    if h + 1 < 8:
                        emit_scores(h + 1)
                    eoff = (l * 8 + h) * 256
                    P.op("dve", lambda e, pp=pp, eoff=eoff: e.tensor_tensor(
                        out=PT[:, pp, 3:5, :], in0=PT[:, pp, 3:5, :],
                        in1=ebt[:, eoff:eoff + 256].rearrange("p (a b) -> p a b", a=2), op=ALU.mult),
                        [("PT", pp), ("ebt",)], [("PT", pp)])
                    vts = [vtile(j) for j in range(5)]

                    def pvmm(e, vts=vts, pp=pp, hh=hh, h=h):
                        for j in range(5):
                            ins = e.matmul(ps[0][:, hh * 65:(hh + 1) * 65], lhsT=PT[:, pp, j, :],
                                           rhs=vts[j][0][:, h * 65:(h + 1) * 65], start=(j == 0), stop=(j == 4))
                        return ins
                    P.op("pe", pvmm, [("PT", pp)] + [v_[1] for v_ in vts], [PSK(0)])
                pv3 = ps[0][:, 0:260].rearrange("p (h c) -> p h c", h=4)
                P.op("dve", lambda e, hg=hg, pv3=pv3: e.tensor_scalar(out=rden[:, hg * 4:(hg + 1) * 4], in0=pv3[:, :, 64],
                                                                      scalar1=1e-30, scalar2=None, op0=ALU.add),
                     [PSK(0)], [("rden", hg)])
                P.op("dve", lambda e, hg=hg: e.reciprocal(out=rden[:, hg * 4:(hg + 1) * 4], in_=rden[:, hg * 4:(hg + 1) * 4]),
                     [("rden", hg)], [("rden", hg)])
                P.op("dve", lambda e, hg=hg, pv3=pv3: e.tensor_tensor(
                    out=oa[:, hg * 4:(hg + 1) * 4, :], in0=pv3[:, :, 0:64],
                    in1=rden[:, hg * 4:(hg + 1) * 4].unsqueeze(2).to_broadcast([128, 4, 64]), op=ALU.mult),
                    [PSK(0), ("rden", hg)], [("oa", hg)])
            c1 = newcol()
            k1 = ("small", c1)
            ss1 = small[:, c1:c1 + 1]
            oaf = oa[:, :, :].rearrange("p h c -> p (h c)")
            P.op("act", lambda e, par=par, ss1=ss1: e.activation(out=mb[:, par, 512:1024], in_=oaf, func=AF.Square, accum_out=ss1),
                 [("oa",)], [("mb", par, 1), k1])
            rstd_chain(ss1, k1, 1, 1.0 / 512)
            P.op("dve", lambda e, par=par, ss1=ss1: e.tensor_scalar(out=mb[:, par, 512:1024], in0=oaf, scalar1=ss1, scalar2=None,
                                                                    op0=ALU.mult), [("oa",), k1], [("mb", par, 1)])

        def merge_part(s):
            par = s % 2
            def trm(e, par=par):
                for cc in range(8):
                    ins = e.transpose(pT[:, cc * 128:(cc + 1) * 128], mb[:, par, cc * 128:(cc + 1) * 128], ident[:, :])
                return ins
            P.op("pe", trm, [("mb", par), ("ident",)], [("pT",)])
            copy_op(evac_eng(), bT[:, :, s * 128:(s + 1) * 128], pT[:, :].rearrange("p (a b) -> p a b", a=8),
                    [("pT",)], [("aT", s)])
        def capture(fn, *a):
            saved = P.op
            lst = []
            P.op = lambda *aa, **kw: lst.append((aa, kw))
            try:
                fn(*a)
            finally:
                P.op = saved
            return lst

        def merged(a, b):
            out, ia, ib = [], 0, 0
            na, nb = len(a), len(b)
            while ia < na or ib < nb:
                if ib >= nb or (ia < na and ia * nb <= ib * na):
                    out.append(a[ia]); ia += 1
                else:
                    out.append(b[ib]); ib += 1
            return out

        if light:
            for s in range(4):
                hgrn_part(s)
        else:
            hgrn_part(0)
            for s in range(4):
                a_ops = capture(attn_part, s)
                b_ops = capture(hgrn_part, s + 1) if s < 3 else []
                for aa, kw in merged(a_ops, b_ops):
                    P.op(*aa, **kw)
                merge_part(s)
        if ctx.get("post_mix") is not None:
            ctx["post_mix"](l)
        if light:
            return

        P.mark("wout")
        for cgo in range(2):
            w, wkey = wchunk(l, 7 + cgo)
            for s in range(4):
                b = nextbank()

                def mm(e, b=b, s=s, w=w):
                    for cc in range(8):
                        ins = e.matmul(ps[b][:, :], lhsT=bT[:, cc, s * 128:(s + 1) * 128], rhs=w[:, cc, :],
                                       start=(cc == 0), stop=(cc == 7))
                    return ins
                P.op("pe", mm, [("aT", s), wkey], [PSK(b)])
                P.op("dve", lambda e, b=b, s=s, cgo=cgo: e.tensor_tensor(
                    out=x[:, s, cgo * 512:(cgo + 1) * 512], in0=ps[b][:, :], in1=x[:, s, cgo * 512:(cgo + 1) * 512], op=ALU.add),
                    [PSK(b), ("x", s, cgo)], [("x", s, cgo)])

        P.mark("mlp")
        for s in range(4):
            norm_T(s, aT, ("aT",))
        for gq in range(8):
            w, wkey = wchunk(l, 9 + gq)
            for ft in range(4):
                b = nextbank()

                def mm(e, b=b, ft=ft, w=w):
                    for dc in range(8):
                        ins = e.matmul(ps[b][:, :], lhsT=w[:, dc, ft * 128:(ft + 1) * 128], rhs=aT[:, dc, :],
                                       start=(dc == 0), stop=(dc == 7))
                    return ins
                P.op("pe", mm, [("aT",), wkey], [PSK(b)])
                fi = gq * 4 + ft
                ri = 0
                P.op("act", lambda e, b=b, ri=ri: e.activation(out=rl[:, ri, :], in_=ps[b][:, :], func=AF.Relu),
                     [PSK(b)], [("rl", ri)])
                P.op("dve", lambda e, fi=fi, ri=ri: e.tensor_tensor(out=hT[:, fi, :], in0=rl[:, ri, :], in1=rl[:, ri, :], op=ALU.mult),
                     [("rl", ri)], [hkey(fi)])
        for cgo in range(2):
            for fg in range(4):
                w, wkey = wchunk(l, 17 + cgo * 4 + fg)
                for s in range(4):
                    def mm(e, s=s, fg=fg, w=w):
                        for fc in range(8):
                            ins = e.matmul(ps[3 + s][:, :], lhsT=hT[:, fg * 8 + fc, s * 128:(s + 1) * 128], rhs=w[:, fc, :],
                                           start=(fg == 0 and fc == 0), stop=(fg == 3 and fc == 7))
                        return ins
                    P.op("pe", mm, [("ar", fg), wkey], [PSK(3 + s)])
            for s in range(4):
                P.op("dve", lambda e, s=s, cgo=cgo: e.tensor_tensor(
                    out=x[:, s, cgo * 512:(cgo + 1) * 512], in0=ps[3 + s][:, :], in1=x[:, s, cgo * 512:(cgo + 1) * 512], op=ALU.add),
                    [PSK(3 + s), ("x", s, cgo)], [("x", s, cgo)])

    def final_norm(s, dst_ap, okey):
        i = 0
        c = newcol()
        sk = ("small", c)
        ssc = small[:, c:c + 1]
        P.op("act", lambda e: e.activation(out=ystg[:, i, :], in_=x[:, s, :], func=AF.Square, accum_out=ssc),
             [("x", s)], [("ystg", i), sk])
        rstd_chain(ssc, sk, 1, 1.0 / D)
        P.op("dve", lambda e: e.scalar_tensor_tensor(out=ystg[:, i, :], in0=x[:, s, :], scalar=ssc, in1=fgn[:, :],
                                                     op0=ALU.mult, op1=ALU.mult), [("x", s), sk, ("fgn",)], [("ystg", i)])
        P.op("act", lambda e: e.dma_start(out=dst_ap, in_=ystg[:, i, :]), [("ystg", i)], [], dma=("ystg", i))

    def kv_k(l, s):
        return kp_o[l, s * 128:(s + 1) * 128, :]

    def kv_v(l, s):
        return vp_o[l, s * 128:(s + 1) * 128, :]

    for T in range(NT):
        for s in range(4):
            r0 = T * 512 + s * 128
            P.op("act", lambda e, s=s, r0=r0: e.dma_start(out=x[:, s, :], in_=xp[r0:r0 + 128, :]), [], [("x", s)], dma=("x", s))
        prev = T < NPREV
        last = T == NT - 1
        ctx = dict(light=False, sample=False, g0=4 * T,
                   vflag=((pflag[:, :], ("pflag",)) if prev else (ones8[:, :], ("ones8",))),
                   kvout=(dict(k=kv_k, v=kv_v) if last else None),
                   S=lambda l, s: (Sst[l][:, :], ("S", l)))
        layer(0, ctx)
        ctx1 = dict(ctx)
        ctx1["light"] = prev
        ctx1["need_kv"] = (T == NPREV - 1)
        layer(1, ctx1)
        if not prev:
            for s in range(4):
                r0 = (T - NPREV) * 512 + s * 128
                final_norm(s, y_o[r0:r0 + 128, :], None)
    for l in range(2):
        P.op("act", lambda e, l=l: e.dma_start(out=sp_o[l].rearrange("h k v -> k h v"),
                                                in_=Sst[l][:, :].rearrange("p (h v) -> p h v", h=4)),
             [("S", l)], [], dma=("So", l))

    for s in range(4):
        P.op("act", lambda e, s=s: e.dma_start(out=x[:, s, :], in_=xs[s, :, :]), [], [("x", s)], dma=("x", s))

    def skv_k(l, s):
        return ks_o[l, s, :, :]

    def skv_v(l, s):
        return vs_o[l, s, :, :]

    stage_k = arena_f[:, 0:2048].rearrange("p (t c) -> p t c", t=4)
    stage_kb = arena[:, 8192:10240].rearrange("p (t c) -> p t c", t=4)

    def load_sample_cache(l, s):
        sl0 = (s % 2) * 4
        P.op("sp", lambda e: e.dma_start(out=stage_k, in_=ck[l, s].rearrange("(t p) c -> p t c", p=128)),
             [], [("ar", 0)], dma=("bstg",))
        copy_op("dve", stage_kb, stage_k, [("ar", 0)], [("ar", 2)])
        for t in range(4):
            def trk(e, t=t):
                for p in range(4):
                    ins = e.transpose(pT[:, p * 128:(p + 1) * 128], stage_kb[:, t, p * 128:(p + 1) * 128], ident[:, :])
                return ins
            P.op("pe", trk, [("ar", 2), ("ident",)], [("pT",)])
            copy_op(evac_eng(), KR[l][:, :, (sl0 + t) * 128:(sl0 + t + 1) * 128],
                    pT[:, 0:512].rearrange("p (a b) -> p a b", a=4), [("pT",)], [("KR", l, sl0 // 4, "t", t)])
        P.op("sp", lambda e: e.dma_start(out=stage_k, in_=cv[l, s].rearrange("(t p) c -> p t c", p=128)),
             [], [("ar", 0)], dma=("bstg",))
        for t in range(4):
            dv = VR[l][:, sl0 + t, :].rearrange("p (h c) -> p h c", h=8)
            copy_op(evac_eng(), dv[:, :, 0:64], stage_k[:, t, :].rearrange("p (h c) -> p h c", h=8),
                    [("ar", 0)], [("VR", l, sl0 + t)])
            P.op("dve", lambda e, dv=dv: e.tensor_copy(out=dv[:, :, 64], in_=ones8[:, :]), [("ones8",)], [("VR", l, sl0 + t)])

    sctx = dict(light=False, sample=True, g0=0, vflag=(sval[:, :], ("sval",)),
                kvout=dict(k=skv_k, v=skv_v), S=lambda l, s: (Ssm[:, s, :], ("ar", 1)),
                pre_mix=load_sample_cache)
    for l in range(2):
        for s in range(4):
            P.op("act", lambda e, l=l, s=s: e.dma_start(out=Ssm[:, s, :].rearrange("p (h v) -> p h v", h=4),
                                                         in_=st0[l, s].rearrange("h k v -> k h v")),
                 [], [("ar", 1)], dma=("Ssm",))
        def store_states(l):
            for s in range(4):
                P.op("act", lambda e, l=l, s=s: e.dma_start(out=ss_o[l, s].rearrange("h k v -> k h v"),
                                                             in_=Ssm[:, s, :].rearrange("p (h v) -> p h v", h=4)),
                     [("ar", 1)], [], dma=("Sso",))
        sctx["post_mix"] = store_states
        layer(l, sctx)
    for s in range(4):
        final_norm(s, ys_o[s, :, :], None)

    if P.marks is not None:
        for m_ in P.marks:
            print("MARK", m_)
    P.finalize()
    nsem = P.emit(nc, es)
    es.close()
    return nc, len(P.ops), nsem


def _consts():
    t = np.arange(128)
    c = t // 64
    tl = t % 64
    same = (c[:, None] == c[None, :])
    M1 = same * ((tl[None, :] <= tl[:, None]).astype(np.float32) - (tl[None, :] <= 31).astype(np.float32))
    m1t = np.ascontiguousarray(M1.T).astype(np.float32)
    ind = np.zeros((128, 8), np.float32)
    for cc in range(2):
        m = (c == cc)
        ind[:, cc * 3 + 0] = m * (tl <= 31)
        ind[:, cc * 3 + 1] = m * (tl > 31)
        ind[:, cc * 3 + 2] = m
    caust = (tl[:, None] <= np.arange(64)[None, :]).astype(np.float32).astype(ml_dtypes.bfloat16)
    ident = np.eye(128, dtype=np.float32).astype(ml_dtypes.bfloat16)
    sval = np.zeros((128, 8), np.float32)
    sval[:16, :] = 1.0
    return m1t, ind, caust, ident, sval


def _bias_tiles(rel_bias):
    j = np.arange(128)[:, None]
    i = np.arange(128)[None, :]
    idx3 = np.maximum(j - 128 - i, -128) + 128
    idx4 = (j - i) + 128
    inval4 = (j >= 64) & (i < 64)
    out = np.empty((128, 2, 8, 2, 128), np.float32)
    for l in range(2):
        for h in range(8):
            out[:, l, h, 0, :] = rel_bias[l, h][idx3]
            b4 = rel_bias[l, h][idx4].copy()
            b4[inval4] = -200.0
            out[:, l, h, 1, :] = b4
    return out.reshape(128, -1)


_CACHE = {}


def _get_prog(NPREV, NOWN):
    key = (NPREV, NOWN)
    if key not in _CACHE:
        _CACHE[key] = build(NPREV, NOWN)
    return _CACHE[key]


def run_cores(seqs, NPREV, NOWN, samples, weights):
    nc, nops, nsem = _get_prog(NPREV, NOWN)
    m1t, ind, caust, ident, sval = _consts()
    f32 = np.float32
    w = weights

    def pl(a):
        return np.ascontiguousarray(a.reshape(2, 8, 128).transpose(2, 0, 1).reshape(128, 16)).astype(f32)

    shared = {
        "lbp": np.ascontiguousarray(w["lb_param"].reshape(1, 1024)).astype(f32),
        "fgain": np.ascontiguousarray(w["final_norm_g"].reshape(1, 1024)).astype(f32),
        "rb0d": np.ascontiguousarray(w["rel_bias"][:, :, 0].reshape(1, 16)).astype(f32),
        "biasm": _bias_tiles(np.asarray(w["rel_bias"], f32)),
        "g1": pl(w["norm1_g"]), "g2": pl(w["norm2_g"]),
        "go": pl(np.concatenate([w["hg_norm_g"], w["att_norm_g"]], axis=1)),
        "sval": sval, "ident": ident, "m1t": m1t, "ind": ind, "caust": caust,
        "w_in": np.ascontiguousarray(w["w_in"], dtype=f32), "w_out": np.ascontiguousarray(w["w_out"], dtype=f32),
        "w_up": np.ascontiguousarray(w["w_up"], dtype=f32), "w_down": np.ascontiguousarray(w["w_down"], dtype=f32),
    }
    in_maps = []
    for c in range(8):
        xpc, pf = seqs[c]
        sm = samples[c]
        xs = np.zeros((4, 128, 1024), f32)
        xs[:, :16, :] = sm["x"]
        m = dict(shared)
        m.update({
            "xp": np.ascontiguousarray(xpc, dtype=f32), "xs": xs,
            "st0": np.ascontiguousarray(sm["state"], dtype=f32),
            "ck": np.ascontiguousarray(sm["ck"].reshape(2, 4, 512, 512), dtype=f32),
            "cv": np.ascontiguousarray(sm["cv"].reshape(2, 4, 512, 512), dtype=f32),
            "pflag": np.full((128, 8), pf, f32),
        })
        in_maps.append(m)
    res = run_bass_kernel_spmd(nc, in_maps, core_ids=list(range(8)))
    return res.results


def kernel(x_prompt, x_sample, state_hgrn, cache_k, cache_v, lb_param, norm1_g, w_in, hg_norm_g, rel_bias,
           att_norm_g, w_out, norm2_g, w_up, w_down, final_norm_g):
    f32 = np.float32
    x_prompt = np.asarray(x_prompt, f32)
    x_sample = np.asarray(x_sample, f32)
    state_hgrn = np.asarray(state_hgrn, f32)
    cache_k = np.asarray(cache_k, f32)
    cache_v = np.asarray(cache_v, f32)
    weights = dict(lb_param=np.asarray(lb_param, f32), norm1_g=np.asarray(norm1_g, f32), w_in=np.asarray(w_in, f32),
                   hg_norm_g=np.asarray(hg_norm_g, f32), rel_bias=np.asarray(rel_bias, f32),
                   att_norm_g=np.asarray(att_norm_g, f32), w_out=np.asarray(w_out, f32), norm2_g=np.asarray(norm2_g, f32),
                   w_up=np.asarray(w_up, f32), w_down=np.asarray(w_down, f32), final_norm_g=np.asarray(final_norm_g, f32))
    B, T, _ = x_prompt.shape
    half = T // 2
    NH = half // 512
    seqs, samples = [], []
    for c in range(8):
        b, second = c // 2, c % 2
        if second:
            seqs.append((x_prompt[b], 1.0))
        else:
            seqs.append((np.concatenate([np.zeros((half, 1024), f32), x_prompt[b, :half]], axis=0), 0.0))
        sl = slice(4 * c, 4 * c + 4)
        samples.append(dict(x=x_sample[sl], state=state_hgrn[:, sl], ck=cache_k[:, sl], cv=cache_v[:, sl]))
    res = run_cores(seqs, NH, NH, samples, weights)
    y_prompt = np.empty((B, T, 1024), f32)
    y_sample = np.empty((32, 16, 1024), f32)
    nsp = np.empty((2, B, 4, 128, 128), f32)
    nkp = np.empty((2, B, 512, 8, 64), f32)
    nvp = np.empty((2, B, 512, 8, 64), f32)
    nss = np.empty((2, 32, 4, 128, 128), f32)
    nks = np.empty((2, 32, 16, 8, 64), f32)
    nvs = np.empty((2, 32, 16, 8, 64), f32)
    for c in range(8):
        r = res[c]
        b, second = c // 2, c % 2
        y_prompt[b, second * half:(second + 1) * half] = r["y"]
        if second:
            nsp[:, b] = r["state_p"]
            nkp[:, b] = r["kc_p"].reshape(2, 512, 8, 64)
            nvp[:, b] = r["vc_p"].reshape(2, 512, 8, 64)
        sl = slice(4 * c, 4 * c + 4)
        y_sample[sl] = r["ysamp"][:, :16, :]
        nss[:, sl] = r["state_s"]
        nks[:, sl] = r["kc_s"][:, :, :16, :].reshape(2, 4, 16, 8, 64)
        nvs[:, sl] = r["vc_s"][:, :, :16, :].reshape(2, 4, 16, 8, 64)
    return (y_prompt, y_sample, nsp, nkp, nvp, nss, nks, nvs)
```

```python
import numpy as np
import ml_dtypes
from contextlib import ExitStack
import concourse.bass as bass
import concourse.mybir as mybir
from concourse.bass_utils import run_bass_kernel_spmd

F32 = mybir.dt.float32
BF16 = mybir.dt.bfloat16
AF = mybir.ActivationFunctionType
ALU = mybir.AluOpType
AX = mybir.AxisListType

D = 1024
DFF = 4096
EPS = 1e-6
NCH = 25
NSLOT = 4


class _Op:
    __slots__ = ("eng", "fn", "deps", "sig", "ord", "waits", "key", "idx", "dma", "ka", "dbg")


class Prog:
    ENGS = ("pe", "act", "dve", "pool", "sp")

    def __init__(self):
        self.ops = []
        self.lastw = {}
        self.readers = {}
        self.children = {}

    def _register(self, key):
        if key in self.children:
            return
        self.children[key] = set()
        for n in range(1, len(key)):
            pre = key[:n]
            if pre not in self.children:
                self._register(pre)
            self.children[pre].add(key)

    def _related(self, key):
        self._register(key)
        yield key
        for n in range(1, len(key)):
            yield key[:n]
        for c in self.children[key]:
            yield c

    limit = None
    marks = None

    def mark(self, name):
        if self.marks is not None:
            self.marks.append((name, len(self.ops)))

    def op(self, eng, fn, reads=(), writes=(), dma=None):
        if self.limit is not None and len(self.ops) >= self.limit:
            return None
        o = _Op()
        o.eng = eng
        o.fn = fn
        o.dma = dma
        o.key = ("dma", dma) if dma is not None else eng
        o.sig = dma is not None
        o.idx = len(self.ops)
        o.dbg = (tuple(reads), tuple(writes))
        writes = list(writes) + [r for r in reads if r[0] == "ps" and r not in writes]
        deps = {}
        for r in reads:
            for k in self._related(r):
                w = self.lastw.get(k)
                if w is not None:
                    deps[w.idx] = w
        for w_ in writes:
            for k in self._related(w_):
                w = self.lastw.get(k)
                if w is not None:
                    deps[w.idx] = w
                for rd in self.readers.get(k, ()):
                    deps[rd.idx] = rd
        o.deps = list(deps.values())
        for r in reads:
            lst = self.readers.setdefault(r, [])
            lst[:] = [q for q in lst if q.key != o.key]
            lst.append(o)
        for w_ in writes:
            self.lastw[w_] = o
            self.readers[w_] = []
        self.ops.append(o)
        return o

    def finalize(self):
        for o in self.ops:
            for d in o.deps:
                d.sig = True
        cnt = {}
        for o in self.ops:
            if o.sig:
                cnt[o.key] = cnt.get(o.key, 0) + 1
                o.ord = cnt[o.key]
            else:
                o.ord = 0
        self.final_counts = cnt
        prev = {e: {} for e in self.ENGS}
        for o in self.ops:
            k = dict(prev[o.eng])
            waits = []
            for d in sorted(o.deps, key=lambda d: -d.idx):
                if d.key == "pe" and o.eng == "pe" and o.dma is None:
                    continue
                if k.get(d.key, 0) >= d.ord:
                    continue
                waits.append((d.key, d.ord))
                for kk, vv in d.ka.items():
                    if k.get(kk, 0) < vv:
                        k[kk] = vv
            o.waits = waits
            prev[o.eng] = k
            ka = dict(k)
            if o.sig:
                ka[o.key] = o.ord
            o.ka = ka

    def emit(self, nc, es):
        keys = []
        for o in self.ops:
            if o.sig and o.key not in keys:
                keys.append(o.key)
        sems = {}
        for i, k in enumerate(keys):
            sems[k] = es.enter_context(nc.semaphore("s%d" % i))
        by = {e: [o for o in self.ops if o.eng == e] for e in self.ENGS}
        block = es.enter_context(nc.Block())

        def body(ename):
            def run(e):
                for o in by[ename]:
                    for (k, v) in o.waits:
                        e.wait_ge(sems[k], v * 16 if isinstance(k, tuple) else v)
                    ins = o.fn(e)
                    if o.sig:
                        ins.then_inc(sems[o.key], 16 if o.dma is not None else 1)
                if ename == "sp":
                    for k, v in self.final_counts.items():
                        if isinstance(k, tuple):
                            e.wait_ge(sems[k], v * 16)
            return run

        block.tensor(body("pe"))
        block.scalar(body("act"))
        block.vector(body("dve"))
        block.gpsimd(body("pool"))
        block.sync(body("sp"))
        return len(keys)


def build(NPREV, NOWN, NS=4):
    assert NS == 4
    nc = bass.Bass("TRN2", target_bir_lowering=False)
    NT = NPREV + NOWN
    P = Prog()
    import os as _os
    if _os.environ.get("KLIMIT"):
        P.limit = int(_os.environ["KLIMIT"])
    if _os.environ.get("KMARKS"):
        P.marks = []
    es = ExitStack()

    def din(name, shape, dt=F32):
        return nc.dram_tensor(name, list(shape), dt, kind="ExternalInput").ap()

    def dout(name, shape, dt=F32):
        return nc.dram_tensor(name, list(shape), dt, kind="ExternalOutput").ap()

    xp = din("xp", [NT * 512, D])
    xs = din("xs", [4, 128, D])
    st0 = din("st0", [2, 4, 4, 128, 128])
    ck = din("ck", [2, 4, 512, 512])
    cv = din("cv", [2, 4, 512, 512])
    lbp = din("lbp", [1, 1024])
    fgain = din("fgain", [1, D])
    rb0d = din("rb0d", [1, 16])
    biasm = din("biasm", [128, 2 * 8 * 2 * 128])
    g1 = din("g1", [128, 16])
    g2 = din("g2", [128, 16])
    go = din("go", [128, 16])
    pflagd = din("pflag", [128, 8])
    svald = din("sval", [128, 8])
    identd = din("ident", [128, 128], BF16)
    m1td = din("m1t", [128, 128])
    indd = din("ind", [128, 8])
    caustd = din("caust", [128, 64], BF16)
    w_in = din("w_in", [2, D, 3584])
    w_out = din("w_out", [2, D, D])
    w_up = din("w_up", [2, D, DFF])
    w_down = din("w_down", [2, DFF, D])

    y_o = dout("y", [NOWN * 512, D])
    ys_o = dout("ysamp", [4, 128, D])
    sp_o = dout("state_p", [2, 4, 128, 128])
    kp_o = dout("kc_p", [2, 512, 512])
    vp_o = dout("vc_p", [2, 512, 512])
    ss_o = dout("state_s", [2, 4, 4, 128, 128])
    ks_o = dout("kc_s", [2, 4, 128, 512])
    vs_o = dout("vc_s", [2, 4, 128, 512])

    wq = nc.dram_tensor("wq", [2 * NCH, 128, 4096], BF16, kind="Internal").ap()

    def sb(name, shape, dt=F32):
        return es.enter_context(nc.sbuf_tensor(name, list(shape), dt))

    x = sb("x", [128, 4, D])
    ystg = sb("ystg", [128, 1, D])
    xn = sb("xn", [128, 2, D], BF16)
    aT = sb("aT", [128, 8, 512], BF16)
    hq = sb("hq", [128, 4, 512])
    sig = sb("sig", [128, 4, 512])
    vh = sb("vh", [128, 4, 512], BF16)
    sg = sb("sg", [128, 4, 512], BF16)
    QT = sb("QT", [128, 4, 512], BF16)
    KR = [sb("KR%d" % l, [128, 4, 1024], BF16) for l in range(2)]
    VR = [sb("VR%d" % l, [128, 8, 520], BF16) for l in range(2)]
    lf = sb("lf", [128, 512])
    e1 = sb("e1", [128, 512])
    e2 = sb("e2", [128, 512])
    qtl = sb("qtl", [128, 512], BF16)
    ktl = sb("ktl", [128, 512], BF16)
    qkT = sb("qkT", [128, 1024], BF16)
    pt = sb("pt", [128, 4, 64], BF16)
    Sst = [sb("S%d" % l, [128, 512]) for l in range(2)]
    smid = sb("smid", [128, 512], BF16)
    stmp = sb("stmp", [128, 512])
    cE = sb("cE", [128, 32])
    PT = sb("PT", [128, 2, 5, 128], BF16)
    oa = sb("oa", [128, 8, 64])
    rden = sb("rden", [128, 8])
    oh = sb("oh", [128, 512])
    mb = sb("mb", [128, 2, D], BF16)
    ko, vo = stmp, oh
    small = sb("small", [128, 64])
    arena = sb("arena", [128, 16384], BF16)
    rl = sb("rl", [128, 1, 512])
    wr = sb("wr", [128, NSLOT, 4096], BF16)
    ident = sb("identb", [128, 128], BF16)
    m1t = sb("m1tb", [128, 128])
    ind = sb("indb", [128, 8])
    caust = sb("caustb", [128, 64], BF16)
    lbt = sb("lbt", [128, 512])
    oml = sb("oml", [128, 2, 512])
    fgn = sb("fgn", [128, D])
    rb0 = sb("rb0", [128, 16])
    enrb = sb("enrb", [128, 16])
    ebt = sb("ebt", [128, 2 * 8 * 2 * 128], BF16)
    g1s = sb("g1s", [128, 16])
    g2s = sb("g2s", [128, 16])
    gos = sb("gos", [128, 16])
    pflag = sb("pflags", [128, 8])
    sval = sb("svals", [128, 8])
    ones8 = sb("ones8", [128, 8])
    epsc = sb("epsc", [128, 8])

    hT = arena[:, :].rearrange("p (a b) -> p a b", a=32)
    arena_f = arena[:, :].bitcast(F32)
    bT = aT
    Ssm = arena_f[:, 2048:4096].rearrange("p (s c) -> p s c", s=4)
    KN = arena[:, 10240:12288].rearrange("p (a b) -> p a b", a=4)
    VN = arena[:, 12288:14368].rearrange("p (a b) -> p a b", a=4)

    def hkey(ft):
        return ("ar", ft // 8, ft % 8)

    ps = [es.enter_context(nc.psum_tensor("ps%d" % b, [128, 512], F32)) for b in (0, 1)]
    pT = es.enter_context(nc.psum_tensor("pT", [128, 1024], BF16))
    ps += [None]
    ps += [es.enter_context(nc.psum_tensor("ps%d" % b, [128, 512], F32)) for b in (3, 4, 5, 6, 7)]
    PSK = lambda b, *r: ("ps", b)

    rot = {"n": 0}

    def nextbank():
        rot["n"] ^= 1
        return rot["n"]

    tog = {"n": 0}

    def evac_eng():
        tog["n"] ^= 1
        return "act" if tog["n"] else "dve"

    def copy_op(eng, out, in_, reads, writes, scale=None):
        if eng == "act":
            if scale is None:
                P.op("act", lambda e: e.activation(out=out, in_=in_, func=AF.Copy), reads, writes)
            else:
                P.op("act", lambda e: e.activation(out=out, in_=in_, func=AF.Copy, scale=scale), reads, writes)
        else:
            if scale is None:
                P.op("dve", lambda e: e.tensor_copy(out=out, in_=in_), reads, writes)
            else:
                P.op("dve", lambda e: e.tensor_scalar(out=out, in0=in_, scalar1=scale, scalar2=None,
                                                      op0=ALU.mult), reads, writes)

    def cload(dst, src, key, q="sp"):
        P.op(q, lambda e: e.dma_start(out=dst, in_=src), [], [key], dma=("c",) + key)

    cload(ident[:, :], identd[:, :], ("ident",))
    cload(m1t[:, :], m1td[:, :], ("m1t",))
    cload(ind[:, :], indd[:, :], ("ind",))
    cload(caust[:, :], caustd[:, :], ("caust",))
    cload(g1s[:, :], g1[:, :], ("g1s",))
    cload(g2s[:, :], g2[:, :], ("g2s",))
    cload(gos[:, :], go[:, :], ("gos",))
    cload(pflag[:, :], pflagd[:, :], ("pflag",))
    cload(sval[:, :], svald[:, :], ("sval",))
    cload(fgn[:, :], bass.AP(fgain.tensor, 0, [[0, 128], [1, D]]), ("fgn",))
    cload(rb0[:, :], bass.AP(rb0d.tensor, 0, [[0, 128], [1, 16]]), ("rb0",))
    cload(oml[:, :, :].rearrange("p a b -> p (a b)"), bass.AP(lbp.tensor, 0, [[0, 128], [1, 1024]]), ("oml",))
    P.op("dve", lambda e: e.tensor_tensor(out=lbt[:, :], in0=oml[:, 1, :], in1=oml[:, 0, :], op=ALU.subtract),
         [("oml",)], [("lbt",)])
    P.op("act", lambda e: e.activation(out=lbt[:, :], in_=lbt[:, :], func=AF.Sigmoid), [("lbt",)], [("lbt",)])
    P.op("dve", lambda e: e.tensor_scalar(out=oml[:, 1, :], in0=lbt[:, :], scalar1=-1.0, scalar2=1.0,
                                          op0=ALU.mult, op1=ALU.add), [("lbt",)], [("oml",)])
    P.op("dve", lambda e: e.memset(ones8[:, :], 1.0), [], [("ones8",)])
    P.op("dve", lambda e: e.memset(epsc[:, :], EPS), [], [("epsc",)])
    P.op("act", lambda e: e.activation(out=enrb[:, :], in_=rb0[:, :], func=AF.Exp, scale=-1.0), [("rb0",)], [("enrb",)])
    for hlf in range(2):
        c0 = hlf * 2048
        P.op("sp", lambda e, c0=c0: e.dma_start(out=arena_f[:, 0:2048], in_=biasm[:, c0:c0 + 2048]),
             [], [("ar", 0)], dma=("bstg",))
        P.op("act", lambda e, c0=c0: e.activation(out=ebt[:, c0:c0 + 2048], in_=arena_f[:, 0:2048], func=AF.Exp),
             [("ar", 0)], [("ebt",)])
        for gi in range(8):
            col = hlf * 8 + gi
            o3 = c0 + gi * 256
            P.op("dve", lambda e, o3=o3, col=col: e.tensor_scalar(
                out=ebt[:, o3:o3 + 128], in0=ebt[:, o3:o3 + 128], scalar1=enrb[:, col:col + 1], scalar2=None, op0=ALU.mult),
                [("ebt",), ("enrb",)], [("ebt",)])
    for l in range(2):
        P.op("dve", lambda e, l=l: e.memset(KR[l][:, :, :].rearrange("p a b -> p (a b)"), 0.0), [], [("KR", l)])
        P.op("dve", lambda e, l=l: e.memset(VR[l][:, :, :].rearrange("p a b -> p (a b)"), 0.0), [], [("VR", l)])
        P.op("dve", lambda e, l=l: e.memset(Sst[l][:, :], 0.0), [], [("S", l)])
    P.op("dve", lambda e: e.memset(PT[:, :, :, :].rearrange("p a b c -> p (a b c)"), 0.0), [], [("PT",)])

    P.mark("prologue")
    pcount = {"n": 0}

    pieces = []

    def conv_piece(src_ap, ncols, gain_ap, dst_ap, dkey):
        pieces.append((src_ap, ncols, gain_ap, dst_ap, dkey))

    def emit_pieces():
        n = len(pieces)
        for t in range(n + 3):
            if t < n:
                src_ap, ncols, gain_ap, dst_ap, dkey = pieces[t]
                i = t % 4
                stg = arena_f[:, i * 2048: i * 2048 + ncols]
                P.op("sp", lambda e, stg=stg, src_ap=src_ap: e.dma_start(out=stg, in_=src_ap), [], [("ar", i)], dma=("pl", i))
            u = t - 3
            if u >= 0:
                src_ap, ncols, gain_ap, dst_ap, dkey = pieces[u]
                i = u % 4
                stg = arena_f[:, i * 2048: i * 2048 + ncols]
                ob = wr[:, i, 0:ncols]
                eng = "act" if i % 2 == 0 else "dve"
                if gain_ap is None:
                    copy_op(eng, ob, stg, [("ar", i)], [("wr", i)])
                else:
                    copy_op(eng, ob, stg, [("ar", i), ("gains",)], [("wr", i)], scale=gain_ap)
                P.op("sp", lambda e, ob=ob, dst_ap=dst_ap: e.dma_start(out=dst_ap, in_=ob), [("wr", i)], [dkey], dma=("pst", i))

    P.op("dve", lambda e: e.tensor_copy(out=small[:, 60:61], in_=g1s[:, 0:1]), [("g1s",), ("g2s",), ("gos",)], [("gains",)])
    for l in range(2):
        for dc in range(8):
            rows = w_in[l, dc * 128:(dc + 1) * 128, :]
            for (c0, ncg) in ((0, 4), (4, 3)):
                dst = wq[l * NCH + c0: l * NCH + c0 + ncg, :, dc * 512:(dc + 1) * 512].rearrange("c p k -> p c k")
                conv_piece(rows[:, c0 * 512:(c0 + ncg) * 512], ncg * 512, g1s[:, l * 8 + dc: l * 8 + dc + 1],
                           dst, ("wq", l, "in", dc, c0))
        for cc in range(8):
            rows = w_out[l, cc * 128:(cc + 1) * 128, :]
            dst = wq[l * NCH + 7: l * NCH + 9, :, cc * 512:(cc + 1) * 512].rearrange("c p k -> p c k")
            conv_piece(rows, 1024, gos[:, l * 8 + cc: l * 8 + cc + 1], dst, ("wq", l, "out", cc))
        for dc in range(8):
            rows = w_up[l, dc * 128:(dc + 1) * 128, :]
            for hf in range(2):
                dst = wq[l * NCH + 9 + hf * 4: l * NCH + 13 + hf * 4, :, dc * 512:(dc + 1) * 512].rearrange("c p k -> p c k")
                conv_piece(rows[:, hf * 2048:(hf + 1) * 2048], 2048, g2s[:, l * 8 + dc: l * 8 + dc + 1],
                           dst, ("wq", l, "up", dc, hf))
        for ffc in range(32):
            rows = w_down[l, ffc * 128:(ffc + 1) * 128, :]
            fg, fc = ffc // 8, ffc % 8
            dst = wq[l * NCH + 17 + fg: l * NCH + 17 + fg + 5: 4, :, fc * 512:(fc + 1) * 512].rearrange("c p k -> p c k")
            conv_piece(rows, 1024, None, dst, ("wq", l, "down", ffc))

    emit_pieces()

    wst = {"n": 0}

    def wchunk(l, ci):
        slot = wst["n"] % NSLOT
        wst["n"] += 1
        P.op("sp", lambda e: e.dma_start(out=wr[:, slot, :], in_=wq[l * NCH + ci, :, :]),
             [("wq", l)], [("wr", slot)], dma=("w", slot))
        return wr[:, slot, :].rearrange("p (a b) -> p a b", a=8), ("wr", slot)

    scol = {"n": 0}

    def newcol(n=1):
        c = scol["n"]
        scol["n"] = (scol["n"] + n) % 48
        if c + n > 48:
            c = 0
            scol["n"] = n
        return c

    def rstd_chain(ss_ap, key, n, inv_n):
        P.op("act", lambda e: e.activation(out=ss_ap, in_=ss_ap, func=AF.Ln, scale=inv_n, bias=epsc[:, 0:1]),
             [key, ("epsc",)], [key])
        P.op("act", lambda e: e.activation(out=ss_ap, in_=ss_ap, func=AF.Exp, scale=-0.5), [key], [key])

    def norm_T(s, dstT, dkey):
        i = s % 2
        c = newcol()
        sk = ("small", c)
        ssc = small[:, c:c + 1]
        P.op("act", lambda e: e.activation(out=xn[:, i, :], in_=x[:, s, :], func=AF.Square, accum_out=ssc),
             [("x", s)], [("xn", i), sk])
        rstd_chain(ssc, sk, 1, 1.0 / D)
        P.op("dve", lambda e: e.tensor_scalar(out=xn[:, i, :], in0=x[:, s, :], scalar1=ssc, scalar2=None,
                                              op0=ALU.mult), [("x", s), sk], [("xn", i)])

        def tr(e):
            for dc in range(8):
                ins = e.transpose(pT[:, dc * 128:(dc + 1) * 128], xn[:, i, dc * 128:(dc + 1) * 128], ident[:, :])
            return ins
        P.op("pe", tr, [("xn", i), ("ident",)], [("pT",)])
        dst = dstT[:, :, s * 128:(s + 1) * 128]
        src = pT[:, :].rearrange("p (a b) -> p a b", a=8)
        copy_op(evac_eng(), dst, src, [("pT",)], [dkey + (s,)])

    def tok_major(w, wkey, s):
        b = nextbank()

        def mm(e):
            for dc in range(8):
                ins = e.matmul(ps[b][:, :], lhsT=aT[:, dc, s * 128:(s + 1) * 128], rhs=w[:, dc, :],
                               start=(dc == 0), stop=(dc == 7))
            return ins
        P.op("pe", mm, [("aT", s), wkey], [PSK(b)])
        return b

    def feat_major(w, wkey, p):
        b = nextbank()

        def mm(e):
            for dc in range(8):
                ins = e.matmul(ps[b][:, :], lhsT=w[:, dc, p * 128:(p + 1) * 128], rhs=aT[:, dc, :],
                               start=(dc == 0), stop=(dc == 7))
            return ins
        P.op("pe", mm, [("aT",), wkey], [PSK(b)])
        return b

    def layer(l, ctx):
        light = ctx["light"]
        sample = ctx["sample"]
        g0 = ctx["g0"]
        P.mark("layer%d g0=%d sample=%s light=%s" % (l, g0, sample, light))

        if sample:
            for s in range(4):
                P.op("dve", lambda e, s=s: e.tensor_scalar(out=x[:, s, :], in0=x[:, s, :], scalar1=sval[:, 0:1], scalar2=None,
                                                           op0=ALU.mult), [("x", s), ("sval",)], [("x", s)])
        for s in range(4):
            norm_T(s, aT, ("aT",))

        if sample:
            def kdst(p):
                return KN[:, p, :], ("ar", 2)

            def vdst(s):
                return VN[:, s, :], ("ar", 3)
        else:
            slot0 = g0 % 8

            def kdst(p):
                return (KR[l][:, p, slot0 * 128:(slot0 + 4) * 128], ("KR", l, slot0 // 4, p))

            def vdst(s):
                return VR[l][:, slot0 + s, :], ("VR", l, slot0 + s)

        for cg in range(7):
            if light and cg in (0, 3, 4):
                continue
            if light and not ctx.get("need_kv", True) and cg in (5, 6):
                continue
            w, wkey = wchunk(l, cg)
            if cg in (4, 5):
                for p in range(4):
                    b = feat_major(w, wkey, p)
                    if cg == 4:
                        copy_op("act", QT[:, p, :], ps[b][:, :], [PSK(b)], [("QT", p)], scale=0.125)
                    else:
                        d, dk = kdst(p)
                        copy_op("dve", d, ps[b][:, :], [PSK(b)], [dk])
                if cg == 5 and ctx["kvout"] is not None:
                    for s in range(4):
                        dst = ctx["kvout"]["k"](l, s)
                        if dst is None:
                            continue
                        b = tok_major(w, wkey, s)
                        copy_op("act", ko[:, :], ps[b][:, :], [PSK(b)], [("stmp",)])
                        P.op("act", lambda e, dst=dst: e.dma_start(out=dst, in_=ko[:, :]), [("stmp",)], [], dma=("ko",))
            else:
                for s in range(4):
                    b = tok_major(w, wkey, s)
                    if cg == 0:
                        copy_op("act", hq[:, s, :], ps[b][:, :], [PSK(b)], [("hq", s)])
                    elif cg == 1:
                        P.op("act", lambda e, b=b, s=s: e.activation(out=sig[:, s, :], in_=ps[b][:, :], func=AF.Sigmoid),
                             [PSK(b)], [("sig", s)])
                    elif cg == 2:
                        copy_op("dve", vh[:, s, :], ps[b][:, :], [PSK(b)], [("vh", s)])
                    elif cg == 3:
                        P.op("act", lambda e, b=b, s=s: e.activation(out=sg[:, s, :], in_=ps[b][:, :], func=AF.Silu),
                             [PSK(b)], [("sg", s)])
                    elif cg == 6:
                        d, dk = vdst(s)
                        dv = d.rearrange("p (h c) -> p h c", h=8)
                        copy_op("dve", dv[:, :, 0:64], ps[b][:, :].rearrange("p (h c) -> p h c", h=8), [PSK(b)], [dk])
                        vf, vfk = ctx["vflag"]
                        P.op("dve", lambda e, dv=dv, vf=vf: e.tensor_copy(out=dv[:, :, 64], in_=vf), [vfk], [dk])
                        if ctx["kvout"] is not None:
                            dst = ctx["kvout"]["v"](l, s)
                            if dst is not None:
                                copy_op("act", vo[:, :], ps[b][:, :], [PSK(b)], [("oh",)])
                                P.op("act", lambda e, dst=dst: e.dma_start(out=dst, in_=vo[:, :]), [("oh",)], [], dma=("vo",))

        P.mark("mixers")
        def hgrn_part(s):
            if ctx.get("pre_mix") is not None:
                ctx["pre_mix"](l, s)
            S_ap, S_key = ctx["S"](l, s)
            par = s % 2
            if l == 1:
                P.op("dve", lambda e, s=s: e.tensor_tensor(out=sig[:, s, :], in0=sig[:, s, :], in1=oml[:, 1, :], op=ALU.mult),
                     [("sig", s), ("oml",)], [("sig", s)])
                P.op("dve", lambda e, s=s: e.tensor_tensor(out=sig[:, s, :], in0=sig[:, s, :], in1=lbt[:, :], op=ALU.add),
                     [("sig", s), ("lbt",)], [("sig", s)])
            P.op("act", lambda e, s=s: e.activation(out=lf[:, :], in_=sig[:, s, :], func=AF.Ln), [("sig", s)], [("lf",)])
            if sample:
                P.op("dve", lambda e: e.tensor_scalar(out=lf[:, :], in0=lf[:, :], scalar1=sval[:, 0:1], scalar2=None,
                                                      op0=ALU.mult), [("lf",), ("sval",)], [("lf",)])
            P.op("dve", lambda e, s=s: e.tensor_scalar(out=sig[:, s, :], in0=sig[:, s, :], scalar1=-1.0, scalar2=1.0,
                                                       op0=ALU.mult, op1=ALU.add), [("sig", s)], [("sig", s)])
            P.op("pe", lambda e: e.matmul(ps[3][:, :], lhsT=m1t[:, :], rhs=lf[:, :], start=True, stop=True),
                 [("m1t",), ("lf",)], [PSK(3)])

            def colmm(e):
                for h in range(4):
                    ins = e.matmul(ps[4][:, 256 + h * 8: 256 + h * 8 + 8], lhsT=lf[:, h * 128:(h + 1) * 128],
                                   rhs=ind[:, 0:8], start=True, stop=True)
                return ins
            P.op("pe", colmm, [("lf",), ("ind",)], [PSK(4, "cols")])
            P.op("act", lambda e: e.activation(out=cE[:, :], in_=ps[4][:, 256:288], func=AF.Exp), [PSK(4, "cols")], [("cE",)])
            P.op("act", lambda e: e.activation(out=e2[:, :], in_=ps[3][:, :], func=AF.Exp, scale=-1.0), [PSK(3)], [("e2",)])
            P.op("dve", lambda e, s=s: e.tensor_tensor(out=ktl[:, :], in0=sig[:, s, :], in1=e2[:, :], op=ALU.mult),
                 [("sig", s), ("e2",)], [("ktl",)])
            if not light:
                P.op("act", lambda e: e.activation(out=e1[:, :], in_=ps[3][:, :], func=AF.Exp), [PSK(3)], [("e1",)])
                P.op("dve", lambda e, s=s: e.tensor_tensor(out=qtl[:, :], in0=hq[:, s, :], in1=e1[:, :], op=ALU.mult),
                     [("hq", s), ("e1",)], [("qtl",)])

                def trq(e):
                    for h in range(4):
                        e.transpose(pT[:, h * 128:(h + 1) * 128], qtl[:, h * 128:(h + 1) * 128], ident[:, :])
                    for h in range(4):
                        ins = e.transpose(pT[:, 512 + h * 128: 512 + (h + 1) * 128], ktl[:, h * 128:(h + 1) * 128], ident[:, :])
                    return ins
                P.op("pe", trq, [("qtl",), ("ktl",), ("ident",)], [("pT",)])
                copy_op("act", qkT[:, :], pT[:, :], [("pT",)], [("qkT",)])

                def scm(e):
                    for h in range(4):
                        for c in range(2):
                            ins = e.matmul(ps[4][c * 64:(c + 1) * 64, h * 64:(h + 1) * 64],
                                           lhsT=qkT[:, 512 + h * 128 + c * 64: 512 + h * 128 + (c + 1) * 64],
                                           rhs=qkT[:, h * 128 + c * 64: h * 128 + (c + 1) * 64],
                                           start=True, stop=True, tile_position=(0, c * 64))
                    return ins
                P.op("pe", scm, [("qkT",)], [PSK(4, "sc")])
                P.op("dve", lambda e: e.tensor_tensor(
                    out=pt[:, :, :], in0=ps[4][:, 0:256].rearrange("p (h t) -> p h t", h=4),
                    in1=caust[:, :].unsqueeze(1).to_broadcast([128, 4, 64]), op=ALU.mult),
                    [PSK(4, "sc"), ("caust",)], [("pt",)])
            cE3 = cE[:, :].rearrange("p (h k) -> p h k", h=4)
            for c in range(2):
                if not light:
                    P.op("dve", lambda e, c=c, S_ap=S_ap: e.tensor_tensor(
                        out=smid[:, :].rearrange("p (h v) -> p h v", h=4), in0=S_ap.rearrange("p (h v) -> p h v", h=4),
                        in1=cE[:, :].rearrange("p (h k) -> p h k", h=4)[:, :, c * 3].unsqueeze(2).to_broadcast([128, 4, 128]),
                        op=ALU.mult), [S_key, ("cE",)], [("smid",)])

                    def omm(e, c=c, s=s):
                        for h in range(4):
                            e.matmul(ps[5][c * 64:(c + 1) * 64, h * 128:(h + 1) * 128],
                                     lhsT=qkT[:, h * 128 + c * 64: h * 128 + (c + 1) * 64],
                                     rhs=smid[:, h * 128:(h + 1) * 128], start=True, stop=False,
                                     tile_position=(0, c * 64))
                            ins = e.matmul(ps[5][c * 64:(c + 1) * 64, h * 128:(h + 1) * 128],
                                           lhsT=pt[c * 64:(c + 1) * 64, h, :],
                                           rhs=vh[c * 64:(c + 1) * 64, s, h * 128:(h + 1) * 128], start=False, stop=True,
                                           tile_position=(c * 64, c * 64))
                        return ins
                    P.op("pe", omm, [("qkT",), ("smid",), ("pt",), ("vh", s)], [PSK(5, c)])

                def sum_(e, c=c, s=s):
                    for h in range(4):
                        ins = e.matmul(ps[6][:, h * 128:(h + 1) * 128], lhsT=ktl[c * 64:(c + 1) * 64, h * 128:(h + 1) * 128],
                                       rhs=vh[c * 64:(c + 1) * 64, s, h * 128:(h + 1) * 128], start=True, stop=True)
                    return ins
                P.op("pe", sum_, [("ktl",), ("vh", s)], [PSK(6)])
                S3 = S_ap.rearrange("p (h v) -> p h v", h=4)
                P.op("dve", lambda e, c=c, S3=S3: e.tensor_tensor(
                    out=stmp[:, :].rearrange("p (h v) -> p h v", h=4), in0=S3,
                    in1=cE3[:, :, c * 3 + 2].unsqueeze(2).to_broadcast([128, 4, 128]), op=ALU.mult),
                    [S_key, ("cE",)], [("stmp",)])
                P.op("dve", lambda e, c=c, S3=S3: e.tensor_tensor(
                    out=S3, in0=ps[6][:, :].rearrange("p (h v) -> p h v", h=4),
                    in1=cE3[:, :, c * 3 + 1].unsqueeze(2).to_broadcast([128, 4, 128]), op=ALU.mult),
                    [PSK(6), ("cE",)], [S_key])
                P.op("dve", lambda e, S_ap=S_ap: e.tensor_tensor(out=S_ap, in0=S_ap, in1=stmp[:, :], op=ALU.add),
                     [S_key, ("stmp",)], [S_key])
            if light:
                return
            c4 = newcol(4)
            k4 = ("small", c4)
            ss4 = small[:, c4:c4 + 4]

            P.op("act", lambda e: e.activation(out=oh[:, :], in_=ps[5][:, :], func=AF.Square), [PSK(5)], [("oh",)])
            P.op("dve", lambda e, ss4=ss4: e.tensor_reduce(out=ss4, in_=oh[:, :].rearrange("p (h v) -> p h v", h=4),
                                                           axis=AX.X, op=ALU.add), [("oh",)], [k4])
            rstd_chain(ss4, k4, 4, 1.0 / 128)
            P.op("dve", lambda e, ss4=ss4: e.tensor_tensor(
                out=oh[:, :].rearrange("p (h v) -> p h v", h=4), in0=ps[5][:, :].rearrange("p (h v) -> p h v", h=4),
                in1=ss4.unsqueeze(2).to_broadcast([128, 4, 128]), op=ALU.mult), [PSK(5), k4], [("oh",)])
            P.op("dve", lambda e, s=s, par=par: e.tensor_tensor(out=mb[:, par, 0:512], in0=oh[:, :], in1=sg[:, s, :], op=ALU.mult),
                 [("oh",), ("sg", s)], [("mb", par, 0)])

        def attn_part(s):
            par = s % 2
            P.mark("attn s=%d" % s)
            g = g0 + s
            if sample:
                def ktile(p, j, s=s):
                    if j < 4:
                        sl = (s % 2) * 4 + j
                        return KR[l][:, p, sl * 128:(sl + 1) * 128], ("KR", l, sl // 4)
                    return KN[:, p, s * 128:(s + 1) * 128], ("ar", 2)

                def vtile(j, s=s):
                    if j < 4:
                        sl = (s % 2) * 4 + j
                        return VR[l][:, sl, :], ("VR", l, sl)
                    return VN[:, s, :], ("ar", 3)
            else:
                def ktile(p, j, g=g):
                    sl = (g - 4 + j) % 8
                    return KR[l][:, p, sl * 128:(sl + 1) * 128], ("KR", l, sl // 4)

                def vtile(j, g=g):
                    sl = (g - 4 + j) % 8
                    return VR[l][:, sl, :], ("VR", l, sl)
            SB = ((7, 1), (7, 1))

            def emit_scores(h, s=s):
                p, eo = h // 2, (h % 2) * 64
                ba, bt = SB[h % 2]
                kts = [ktile(p, j) for j in range(5)]

                def scmm(e, kts=kts, p=p, eo=eo, s=s, ba=ba, bt=bt):
                    for j in range(4):
                        e.matmul(ps[ba][:, j * 128:(j + 1) * 128], lhsT=kts[j][0][eo:eo + 64, :],
                                 rhs=QT[eo:eo + 64, p, s * 128:(s + 1) * 128], start=True, stop=True)
                    return e.matmul(ps[bt][:, 0:128], lhsT=kts[4][0][eo:eo + 64, :],
                                    rhs=QT[eo:eo + 64, p, s * 128:(s + 1) * 128], start=True, stop=True)
                P.op("pe", scmm, [k_[1] for k_ in kts] + [("QT", p)], [PSK(ba), PSK(bt)])

            emit_scores(0)
            for hg in range(2):
                for hh in range(4):
                    h = hg * 4 + hh
                    pp = h % 2
                    ba, bt = SB[h % 2]
                    bcol = l * 8 + h

                    def exps(e, pp=pp, bcol=bcol, ba=ba, bt=bt):
                        e.activation(out=PT[:, pp, 0:4, :], in_=ps[ba][:, 0:512].rearrange("p (a b) -> p a b", a=4),
                                     func=AF.Exp, bias=rb0[:, bcol:bcol + 1])
                        return e.activation(out=PT[:, pp, 4, :], in_=ps[bt][:, 0:128], func=AF.Exp)
                    P.op("act", exps, [PSK(ba), PSK(bt), ("rb0",)], [("PT", pp)])
                    P.op("dve", lambda e, pp=pp: e.memset(PT[0:64, pp, 0, 64:128], 0.0), [("PT", pp)], [("PT", pp)])
                    if h + 1 < 8:
                        emit_scores(h + 1)
                    eoff = (l * 8 + h) * 256
                    P.op("dve", lambda e, pp=pp, eoff=eoff: e.tensor_tensor(
                        out=PT[:, pp, 3:5, :], in0=PT[:, pp, 3:5, :],
                        in1=ebt[:, eoff:eoff + 256].rearrange("p (a b) -> p a b", a=2), op=ALU.mult),
                        [("PT", pp), ("ebt",)], [("PT", pp)])
                    vts = [vtile(j) for j in range(5)]

                    def pvmm(e, vts=vts, pp=pp, hh=hh, h=h):
                        for j in range(5):
                            ins = e.matmul(ps[0][:, hh * 65:(hh + 1) * 65], lhsT=PT[:, pp, j, :],
                                           rhs=vts[j][0][:, h * 65:(h + 1) * 65], start=(j == 0), stop=(j == 4))
                        return ins
                    P.op("pe", pvmm, [("PT", pp)] + [v_[1] for v_ in vts], [PSK(0)])
                pv3 = ps[0][:, 0:260].rearrange("p (h c) -> p h c", h=4)
                P.op("dve", lambda e, hg=hg, pv3=pv3: e.tensor_scalar(out=rden[:, hg * 4:(hg + 1) * 4], in0=pv3[:, :, 64],
                                                                      scalar1=1e-30, scalar2=None, op0=ALU.add),
                     [PSK(0)], [("rden", hg)])
                P.op("dve", lambda e, hg=hg: e.reciprocal(out=rden[:, hg * 4:(hg + 1) * 4], in_=rden[:, hg * 4:(hg + 1) * 4]),
                     [("rden", hg)], [("rden", hg)])
                P.op("dve", lambda e, hg=hg, pv3=pv3: e.tensor_tensor(
                    out=oa[:, hg * 4:(hg + 1) * 4, :], in0=pv3[:, :, 0:64],
                    in1=rden[:, hg * 4:(hg + 1) * 4].unsqueeze(2).to_broadcast([128, 4, 64]), op=ALU.mult),
                    [PSK(0), ("rden", hg)], [("oa", hg)])
            c1 = newcol()
            k1 = ("small", c1)
            ss1 = small[:, c1:c1 + 1]
            oaf = oa[:, :, :].rearrange("p h c -> p (h c)")
            P.op("act", lambda e, par=par, ss1=ss1: e.activation(out=mb[:, par, 512:1024], in_=oaf, func=AF.Square, accum_out=ss1),
                 [("oa",)], [("mb", par, 1), k1])
            rstd_chain(ss1, k1, 1, 1.0 / 512)
            P.op("dve", lambda e, par=par, ss1=ss1: e.tensor_scalar(out=mb[:, par, 512:1024], in0=oaf, scalar1=ss1, scalar2=None,
                                                                    op0=ALU.mult), [("oa",), k1], [("mb", par, 1)])

        def merge_part(s):
            par = s % 2
            def trm(e, par=par):
                for cc in range(8):
                    ins = e.transpose(pT[:, cc * 128:(cc + 1) * 128], mb[:, par, cc * 128:(cc + 1) * 128], ident[:, :])
                return ins
            P.op("pe", trm, [("mb", par), ("ident",)], [("pT",)])
            copy_op(evac_eng(), bT[:, :, s * 128:(s + 1) * 128], pT[:, :].rearrange("p (a b) -> p a b", a=8),
                    [("pT",)], [("aT", s)])
        def capture(fn, *a):
            saved = P.op
            lst = []
            P.op = lambda *aa, **kw: lst.append((aa, kw))
            try:
                fn(*a)
            finally:
                P.op = saved
            return lst

        def merged(a, b):
            out, ia, ib = [], 0, 0
            na, nb = len(a), len(b)
            while ia < na or ib < nb:
                if ib >= nb or (ia < na and ia * nb <= ib * na):
                    out.append(a[ia]); ia += 1
                else:
                    out.append(b[ib]); ib += 1
            return out

        if light:
            for s in range(4):
                hgrn_part(s)
        else:
            hgrn_part(0)
            for s in range(4):
                a_ops = capture(attn_part, s)
                b_ops = capture(hgrn_part, s + 1) if s < 3 else []
                for aa, kw in merged(a_ops, b_ops):
                    P.op(*aa, **kw)
                merge_part(s)
        if ctx.get("post_mix") is not None:
            ctx["post_mix"](l)
        if light:
            return

        P.mark("wout")
        for cgo in range(2):
            w, wkey = wchunk(l, 7 + cgo)
            for s in range(4):
                b = nextbank()

                def mm(e, b=b, s=s, w=w):
                    for cc in range(8):
                        ins = e.matmul(ps[b][:, :], lhsT=bT[:, cc, s * 128:(s + 1) * 128], rhs=w[:, cc, :],
                                       start=(cc == 0), stop=(cc == 7))
                    return ins
                P.op("pe", mm, [("aT", s), wkey], [PSK(b)])
                P.op("dve", lambda e, b=b, s=s, cgo=cgo: e.tensor_tensor(
                    out=x[:, s, cgo * 512:(cgo + 1) * 512], in0=ps[b][:, :], in1=x[:, s, cgo * 512:(cgo + 1) * 512], op=ALU.add),
                    [PSK(b), ("x", s, cgo)], [("x", s, cgo)])

        P.mark("mlp")
        for s in range(4):
            norm_T(s, aT, ("aT",))
        for gq in range(8):
            w, wkey = wchunk(l, 9 + gq)
            for ft in range(4):
                b = nextbank()

                def mm(e, b=b, ft=ft, w=w):
                    for dc in range(8):
                        ins = e.matmul(ps[b][:, :], lhsT=w[:, dc, ft * 128:(ft + 1) * 128], rhs=aT[:, dc, :],
                                       start=(dc == 0), stop=(dc == 7))
                    return ins
                P.op("pe", mm, [("aT",), wkey], [PSK(b)])
                fi = gq * 4 + ft
                ri = 0
                P.op("act", lambda e, b=b, ri=ri: e.activation(out=rl[:, ri, :], in_=ps[b][:, :], func=AF.Relu),
                     [PSK(b)], [("rl", ri)])
                P.op("dve", lambda e, fi=fi, ri=ri: e.tensor_tensor(out=hT[:, fi, :], in0=rl[:, ri, :], in1=rl[:, ri, :], op=ALU.mult),
                     [("rl", ri)], [hkey(fi)])
        for cgo in range(2):
            for fg in range(4):
                w, wkey = wchunk(l, 17 + cgo * 4 + fg)
                for s in range(4):
                    def mm(e, s=s, fg=fg, w=w):
                        for fc in range(8):
                            ins = e.matmul(ps[3 + s][:, :], lhsT=hT[:, fg * 8 + fc, s * 128:(s + 1) * 128], rhs=w[:, fc, :],
                                           start=(fg == 0 and fc == 0), stop=(fg == 3 and fc == 7))
                        return ins
                    P.op("pe", mm, [("ar", fg), wkey], [PSK(3 + s)])
            for s in range(4):
                P.op("dve", lambda e, s=s, cgo=cgo: e.tensor_tensor(
                    out=x[:, s, cgo * 512:(cgo + 1) * 512], in0=ps[3 + s][:, :], in1=x[:, s, cgo * 512:(cgo + 1) * 512], op=ALU.add),
                    [PSK(3 + s), ("x", s, cgo)], [("x", s, cgo)])

    def final_norm(s, dst_ap, okey):
        i = 0
        c = newcol()
        sk = ("small", c)
        ssc = small[:, c:c + 1]
        P.op("act", lambda e: e.activation(out=ystg[:, i, :], in_=x[:, s, :], func=AF.Square, accum_out=ssc),
             [("x", s)], [("ystg", i), sk])
        rstd_chain(ssc, sk, 1, 1.0 / D)
        P.op("dve", lambda e: e.scalar_tensor_tensor(out=ystg[:, i, :], in0=x[:, s, :], scalar=ssc, in1=fgn[:, :],
                                                     op0=ALU.mult, op1=ALU.mult), [("x", s), sk, ("fgn",)], [("ystg", i)])
        P.op("act", lambda e: e.dma_start(out=dst_ap, in_=ystg[:, i, :]), [("ystg", i)], [], dma=("ystg", i))

    def kv_k(l, s):
        return kp_o[l, s * 128:(s + 1) * 128, :]

    def kv_v(l, s):
        return vp_o[l, s * 128:(s + 1) * 128, :]

    for T in range(NT):
        for s in range(4):
            r0 = T * 512 + s * 128
            P.op("act", lambda e, s=s, r0=r0: e.dma_start(out=x[:, s, :], in_=xp[r0:r0 + 128, :]), [], [("x", s)], dma=("x", s))
        prev = T < NPREV
        last = T == NT - 1
        ctx = dict(light=False, sample=False, g0=4 * T,
                   vflag=((pflag[:, :], ("pflag",)) if prev else (ones8[:, :], ("ones8",))),
                   kvout=(dict(k=kv_k, v=kv_v) if last else None),
                   S=lambda l, s: (Sst[l][:, :], ("S", l)))
        layer(0, ctx)
        ctx1 = dict(ctx)
        ctx1["light"] = prev
        ctx1["need_kv"] = (T == NPREV - 1)
        layer(1, ctx1)
        if not prev:
            for s in range(4):
                r0 = (T - NPREV) * 512 + s * 128
                final_norm(s, y_o[r0:r0 + 128, :], None)
    for l in range(2):
        P.op("act", lambda e, l=l: e.dma_start(out=sp_o[l].rearrange("h k v -> k h v"),
                                                in_=Sst[l][:, :].rearrange("p (h v) -> p h v", h=4)),
             [("S", l)], [], dma=("So", l))

    for s in range(4):
        P.op("act", lambda e, s=s: e.dma_start(out=x[:, s, :], in_=xs[s, :, :]), [], [("x", s)], dma=("x", s))

    def skv_k(l, s):
        return ks_o[l, s, :, :]

    def skv_v(l, s):
        return vs_o[l, s, :, :]

    stage_k = arena_f[:, 0:2048].rearrange("p (t c) -> p t c", t=4)
    stage_kb = arena[:, 8192:10240].rearrange("p (t c) -> p t c", t=4)

    def load_sample_cache(l, s):
        sl0 = (s % 2) * 4
        P.op("sp", lambda e: e.dma_start(out=stage_k, in_=ck[l, s].rearrange("(t p) c -> p t c", p=128)),
             [], [("ar", 0)], dma=("bstg",))
        copy_op("dve", stage_kb, stage_k, [("ar", 0)], [("ar", 2)])
        for t in range(4):
            def trk(e, t=t):
                for p in range(4):
                    ins = e.transpose(pT[:, p * 128:(p + 1) * 128], stage_kb[:, t, p * 128:(p + 1) * 128], ident[:, :])
                return ins
            P.op("pe", trk, [("ar", 2), ("ident",)], [("pT",)])
            copy_op(evac_eng(), KR[l][:, :, (sl0 + t) * 128:(sl0 + t + 1) * 128],
                    pT[:, 0:512].rearrange("p (a b) -> p a b", a=4), [("pT",)], [("KR", l, sl0 // 4, "t", t)])
        P.op("sp", lambda e: e.dma_start(out=stage_k, in_=cv[l, s].rearrange("(t p) c -> p t c", p=128)),
             [], [("ar", 0)], dma=("bstg",))
        for t in range(4):
            dv = VR[l][:, sl0 + t, :].rearrange("p (h c) -> p h c", h=8)
            copy_op(evac_eng(), dv[:, :, 0:64], stage_k[:, t, :].rearrange("p (h c) -> p h c", h=8),
                    [("ar", 0)], [("VR", l, sl0 + t)])
            P.op("dve", lambda e, dv=dv: e.tensor_copy(out=dv[:, :, 64], in_=ones8[:, :]), [("ones8",)], [("VR", l, sl0 + t)])

    sctx = dict(light=False, sample=True, g0=0, vflag=(sval[:, :], ("sval",)),
                kvout=dict(k=skv_k, v=skv_v), S=lambda l, s: (Ssm[:, s, :], ("ar", 1)),
                pre_mix=load_sample_cache)
    for l in range(2):
        for s in range(4):
            P.op("act", lambda e, l=l, s=s: e.dma_start(out=Ssm[:, s, :].rearrange("p (h v) -> p h v", h=4),
                                                         in_=st0[l, s].rearrange("h k v -> k h v")),
                 [], [("ar", 1)], dma=("Ssm",))
        def store_states(l):
            for s in range(4):
                P.op("act", lambda e, l=l, s=s: e.dma_start(out=ss_o[l, s].rearrange("h k v -> k h v"),
                                                             in_=Ssm[:, s, :].rearrange("p (h v) -> p h v", h=4)),
                     [("ar", 1)], [], dma=("Sso",))
        sctx["post_mix"] = store_states
        layer(l, sctx)
    for s in range(4):
        final_norm(s, ys_o[s, :, :], None)

    if P.marks is not None:
        for m_ in P.marks:
            print("MARK", m_)
    P.finalize()
    nsem = P.emit(nc, es)
    es.close()
    return nc, len(P.ops), nsem


def _consts():
    t = np.arange(128)
    c = t // 64
    tl = t % 64
    same = (c[:, None] == c[None, :])
    M1 = same * ((tl[None, :] <= tl[:, None]).astype(np.float32) - (tl[None, :] <= 31).astype(np.float32))
    m1t = np.ascontiguousarray(M1.T).astype(np.float32)
    ind = np.zeros((128, 8), np.float32)
    for cc in range(2):
        m = (c == cc)
        ind[:, cc * 3 + 0] = m * (tl <= 31)
        ind[:, cc * 3 + 1] = m * (tl > 31)
        ind[:, cc * 3 + 2] = m
    caust = (tl[:, None] <= np.arange(64)[None, :]).astype(np.float32).astype(ml_dtypes.bfloat16)
    ident = np.eye(128, dtype=np.float32).astype(ml_dtypes.bfloat16)
    sval = np.zeros((128, 8), np.float32)
    sval[:16, :] = 1.0
    return m1t, ind, caust, ident, sval


def _bias_tiles(rel_bias):
    j = np.arange(128)[:, None]
    i = np.arange(128)[None, :]
    idx3 = np.maximum(j - 128 - i, -128) + 128
    idx4 = (j - i) + 128
    inval4 = (j >= 64) & (i < 64)
    out = np.empty((128, 2, 8, 2, 128), np.float32)
    for l in range(2):
        for h in range(8):
            out[:, l, h, 0, :] = rel_bias[l, h][idx3]
            b4 = rel_bias[l, h][idx4].copy()
            b4[inval4] = -200.0
            out[:, l, h, 1, :] = b4
    return out.reshape(128, -1)


_CACHE = {}


def _get_prog(NPREV, NOWN):
    key = (NPREV, NOWN)
    if key not in _CACHE:
        _CACHE[key] = build(NPREV, NOWN)
    return _CACHE[key]


def run_cores(seqs, NPREV, NOWN, samples, weights):
    nc, nops, nsem = _get_prog(NPREV, NOWN)
    m1t, ind, caust, ident, sval = _consts()
    f32 = np.float32
    w = weights

    def pl(a):
        return np.ascontiguousarray(a.reshape(2, 8, 128).transpose(2, 0, 1).reshape(128, 16)).astype(f32)

    shared = {
        "lbp": np.ascontiguousarray(w["lb_param"].reshape(1, 1024)).astype(f32),
        "fgain": np.ascontiguousarray(w["final_norm_g"].reshape(1, 1024)).astype(f32),
        "rb0d": np.ascontiguousarray(w["rel_bias"][:, :, 0].reshape(1, 16)).astype(f32),
        "biasm": _bias_tiles(np.asarray(w["rel_bias"], f32)),
        "g1": pl(w["norm1_g"]), "g2": pl(w["norm2_g"]),
        "go": pl(np.concatenate([w["hg_norm_g"], w["att_norm_g"]], axis=1)),
        "sval": sval, "ident": ident, "m1t": m1t, "ind": ind, "caust": caust,
        "w_in": np.ascontiguousarray(w["w_in"], dtype=f32), "w_out": np.ascontiguousarray(w["w_out"], dtype=f32),
        "w_up": np.ascontiguousarray(w["w_up"], dtype=f32), "w_down": np.ascontiguousarray(w["w_down"], dtype=f32),
    }
    in_maps = []
    for c in range(8):
        xpc, pf = seqs[c]
        sm = samples[c]
        xs = np.zeros((4, 128, 1024), f32)
        xs[:, :16, :] = sm["x"]
        m = dict(shared)
        m.update({
            "xp": np.ascontiguousarray(xpc, dtype=f32), "xs": xs,
            "st0": np.ascontiguousarray(sm["state"], dtype=f32),
            "ck": np.ascontiguousarray(sm["ck"].reshape(2, 4, 512, 512), dtype=f32),
            "cv": np.ascontiguousarray(sm["cv"].reshape(2, 4, 512, 512), dtype=f32),
            "pflag": np.full((128, 8), pf, f32),
        })
        in_maps.append(m)
    res = run_bass_kernel_spmd(nc, in_maps, core_ids=list(range(8)))
    return res.results


def kernel(x_prompt, x_sample, state_hgrn, cache_k, cache_v, lb_param, norm1_g, w_in, hg_norm_g, rel_bias,
           att_norm_g, w_out, norm2_g, w_up, w_down, final_norm_g):
    f32 = np.float32
    x_prompt = np.asarray(x_prompt, f32)
    x_sample = np.asarray(x_sample, f32)
    state_hgrn = np.asarray(state_hgrn, f32)
    cache_k = np.asarray(cache_k, f32)
    cache_v = np.asarray(cache_v, f32)
    weights = dict(lb_param=np.asarray(lb_param, f32), norm1_g=np.asarray(norm1_g, f32), w_in=np.asarray(w_in, f32),
                   hg_norm_g=np.asarray(hg_norm_g, f32), rel_bias=np.asarray(rel_bias, f32),
                   att_norm_g=np.asarray(att_norm_g, f32), w_out=np.asarray(w_out, f32), norm2_g=np.asarray(norm2_g, f32),
                   w_up=np.asarray(w_up, f32), w_down=np.asarray(w_down, f32), final_norm_g=np.asarray(final_norm_g, f32))
    B, T, _ = x_prompt.shape
    half = T // 2
    NH = half // 512
    seqs, samples = [], []
    for c in range(8):
        b, second = c // 2, c % 2
        if second:
            seqs.append((x_prompt[b], 1.0))
        else:
            seqs.append((np.concatenate([np.zeros((half, 1024), f32), x_prompt[b, :half]], axis=0), 0.0))
        sl = slice(4 * c, 4 * c + 4)
        samples.append(dict(x=x_sample[sl], state=state_hgrn[:, sl], ck=cache_k[:, sl], cv=cache_v[:, sl]))
    res = run_cores(seqs, NH, NH, samples, weights)
    y_prompt = np.empty((B, T, 1024), f32)
    y_sample = np.empty((32, 16, 1024), f32)
    nsp = np.empty((2, B, 4, 128, 128), f32)
    nkp = np.empty((2, B, 512, 8, 64), f32)
    nvp = np.empty((2, B, 512, 8, 64), f32)
    nss = np.empty((2, 32, 4, 128, 128), f32)
    nks = np.empty((2, 32, 16, 8, 64), f32)
    nvs = np.empty((2, 32, 16, 8, 64), f32)
    for c in range(8):
        r = res[c]
        b, second = c // 2, c % 2
        y_prompt[b, second * half:(second + 1) * half] = r["y"]
        if second:
            nsp[:, b] = r["state_p"]
            nkp[:, b] = r["kc_p"].reshape(2, 512, 8, 64)
            nvp[:, b] = r["vc_p"].reshape(2, 512, 8, 64)
        sl = slice(4 * c, 4 * c + 4)
        y_sample[sl] = r["ysamp"][:, :16, :]
        nss[:, sl] = r["state_s"]
        nks[:, sl] = r["kc_s"][:, :, :16, :].reshape(2, 4, 16, 8, 64)
        nvs[:, sl] = r["vc_s"][:, :, :16, :].reshape(2, 4, 16, 8, 64)
    return (y_prompt, y_sample, nsp, nkp, nvp, nss, nks, nvs)
```
